# Optimizing a Trainium2 kernel written in Bass

```python
import math
import jax, jax.numpy as jnp
from jax import lax
import numpy as np

D_MODEL = 1024
BATCH = 8
SEQ = 4096
DEPTH = 2

HEAD_DIM = 64
A_Q_HEADS = 8
A_KV_HEADS = 2
WINDOW = 128
ROPE_DIM = HEAD_DIM // 4
ROPE_THETA = 500000.0
B_HEADS = 4
B_DK = 64
B_DV = 64
C_HEADS = 4
C_DK = 64
C_DV = 64
CONV_WIDTH = 4
CHUNK = 64
D_FF = -(-8 * D_MODEL // (3 * 256)) * 256
D_MIX = A_Q_HEADS * HEAD_DIM + B_HEADS * B_DV + C_HEADS * C_DV
C_CONV_CH = C_HEADS * (2 * C_DK + C_DV)
IN_SIZES = (A_Q_HEADS * HEAD_DIM, A_KV_HEADS * HEAD_DIM, A_KV_HEADS * HEAD_DIM,
            B_HEADS * B_DK, B_HEADS * B_DK, B_HEADS * B_DV, B_HEADS * B_DV,
            C_CONV_CH, C_HEADS * C_DV, C_HEADS, C_HEADS)
D_IN = sum(IN_SIZES)
NORM_EPS = 1e-6

kernel_name = 'hybrid_swa_hgrn2_gdn_block'


def rmsnorm(x, gain):
    xf = x.astype(jnp.float32)
    y = xf * lax.rsqrt(jnp.mean(xf * xf, axis=-1, keepdims=True) + NORM_EPS)
    return (y * gain.astype(jnp.float32)).astype(x.dtype)


def l2norm(x):
    return x * lax.rsqrt(jnp.sum(x * x, axis=-1, keepdims=True) + NORM_EPS)


def partial_rope(x, positions):
    half = ROPE_DIM // 2
    inv_freq = ROPE_THETA ** (-jnp.arange(half, dtype=jnp.float32) * 2.0 / ROPE_DIM)
    ang = positions.astype(jnp.float32)[..., None] * inv_freq
    cos = jnp.cos(ang)[:, :, None, :]
    sin = jnp.sin(ang)[:, :, None, :]
    xf = x.astype(jnp.float32)
    x1, x2, xp = xf[..., :half], xf[..., half:ROPE_DIM], xf[..., ROPE_DIM:]
    out = jnp.concatenate([x1 * cos - x2 * sin, x2 * cos + x1 * sin, xp], axis=-1)
    return out.astype(x.dtype)


def sliding_window_attention(q, k, v, sinks):
    b, s, hq, d = q.shape
    hkv = k.shape[2]
    grp = hq // hkv
    nb = s // WINDOW
    qb = q.reshape(b, nb, WINDOW, hkv, grp, d)

    def band(t):
        tb = t.reshape(b, nb, WINDOW, hkv, d)
        prev = jnp.concatenate([jnp.zeros_like(tb[:, :1]), tb[:, :-1]], axis=1)
        return jnp.concatenate([prev, tb], axis=2)

    kb, vb = band(k), band(v)
    scores = jnp.einsum('bnqhgd,bnkhd->bnhgqk', qb, kb).astype(jnp.float32) * (d ** -0.5)
    qi = jnp.arange(WINDOW)[:, None]
    kj = jnp.arange(2 * WINDOW)[None, :]
    delta = qi + WINDOW - kj
    blk = jnp.arange(nb)[:, None, None]
    valid = (delta >= 0) & (delta < WINDOW) & (blk * WINDOW + kj - WINDOW >= 0)
    scores = jnp.where(valid[None, :, None, None], scores, -jnp.inf)
    sink = sinks.astype(jnp.float32).reshape(hkv, grp)[None, None, :, :, None, None]
    m = jnp.maximum(scores.max(axis=-1, keepdims=True), sink)
    p = jnp.exp(scores - m)
    denom = p.sum(axis=-1, keepdims=True) + jnp.exp(sink - m)
    out = jnp.einsum('bnhgqk,bnkhd->bnqhgd', (p / denom).astype(v.dtype), vb)
    return out.reshape(b, s, hq * d)


def hgrn2_chunked(q, log_f, v):
    b, s, h, dk = q.shape
    dv = v.shape[-1]
    nc = s // CHUNK

    def to_chunks(t):
        return t.reshape(b, nc, CHUNK, h, t.shape[-1]).transpose(1, 0, 3, 2, 4)

    causal = jnp.tril(jnp.ones((CHUNK, CHUNK), dtype=bool))

    def step(state, inp):
        qt, lf, vt = inp
        kt = -jnp.expm1(lf)
        cum = jnp.cumsum(lf, axis=2)
        diff = cum[:, :, :, None, :] - cum[:, :, None, :, :]
        decay = jnp.exp(jnp.where(causal[:, :, None], diff, -jnp.inf))
        attn = jnp.einsum('bhtd,bhsd,bhtsd->bhts', qt, kt, decay)
        o = attn @ vt + jnp.einsum('bhtd,bhde->bhte', qt * jnp.exp(cum), state)
        last = cum[:, :, -1:, :]
        state = jnp.exp(last[:, :, 0, :, None]) * state + jnp.einsum('bhsd,bhse->bhde', kt * jnp.exp(last - cum), vt)
        return state, o

    state0 = jnp.zeros((b, h, dk, dv), jnp.float32)
    _, o = lax.scan(step, state0, (to_chunks(q), to_chunks(log_f), to_chunks(v)))
    return o.transpose(1, 0, 3, 2, 4).reshape(b, s, h, dv)


def causal_depthwise_conv(x, w):
    ch = x.shape[-1]
    return lax.conv_general_dilated(x, w[:, None, :].astype(x.dtype), window_strides=(1,),
                                    padding=[(CONV_WIDTH - 1, 0)],
                                    dimension_numbers=('NWC', 'WIO', 'NWC'),
                                    feature_group_count=ch)


def gated_delta_chunked(q, k, v, beta, g):
    b, s, h, dk = q.shape
    dv = v.shape[-1]
    nc = s // CHUNK

    def to_chunks(t):
        return jnp.swapaxes(t.reshape((b, nc, CHUNK) + t.shape[2:]), 2, 3)

    qc, kc, vc, bc, gc = (to_chunks(t) for t in (q, k, v, beta, g))
    G = jnp.cumsum(gc, axis=-1)
    incl = jnp.tril(jnp.ones((CHUNK, CHUNK), dtype=bool))
    strict = jnp.tril(jnp.ones((CHUNK, CHUNK), dtype=bool), k=-1)
    L = jnp.exp(jnp.where(incl, G[..., :, None] - G[..., None, :], -jnp.inf))
    kb = kc * bc[..., None]
    A = jnp.where(strict, jnp.einsum('bnhtd,bnhsd->bnhts', kb, kc) * L, 0.0)
    eye = jnp.eye(CHUNK, dtype=jnp.float32)
    T = lax.linalg.triangular_solve(eye + A, jnp.broadcast_to(eye, A.shape), left_side=True,
                                    lower=True, unit_diagonal=True)
    U = T @ (vc * bc[..., None])
    W = T @ (kb * jnp.exp(G)[..., None])
    qk = jnp.where(incl, jnp.einsum('bnhtd,bnhsd->bnhts', qc, kc) * L, 0.0)
    q_dec = qc * jnp.exp(G)[..., None]
    k_dec = kc * jnp.exp(G[..., -1:] - G)[..., None]
    g_last = jnp.exp(G[..., -1])

    def step(state, inp):
        u, w, qd, kd, qkc, gl = inp
        v_new = u - w @ state
        o = qd @ state + qkc @ v_new
        state = gl[..., None, None] * state + jnp.swapaxes(kd, -1, -2) @ v_new
        return state, o

    xs = tuple(jnp.moveaxis(t, 1, 0) for t in (U, W, q_dec, k_dec, qk, g_last))
    state0 = jnp.zeros((b, h, dk, dv), jnp.float32)
    _, o = lax.scan(step, state0, xs)
    return jnp.swapaxes(jnp.moveaxis(o, 0, 1), 2, 3).reshape(b, s, h, dv)


def hybrid_mixer(h, positions, w_in, q_norm, k_norm, sinks, lb, hgrn_norm, conv_w, a_log, dt_bias,
                 gdn_norm, w_out):
    b, s, _ = h.shape
    f32 = jnp.float32
    proj = h @ w_in
    split_at = [int(i) for i in np.cumsum(IN_SIZES)[:-1]]
    aq, ak, av, bq, bf, bv, bg, cqkv, cg, cb, ca = jnp.split(proj, split_at, axis=-1)

    aq = partial_rope(rmsnorm(aq.reshape(b, s, A_Q_HEADS, HEAD_DIM), q_norm), positions)
    ak = partial_rope(rmsnorm(ak.reshape(b, s, A_KV_HEADS, HEAD_DIM), k_norm), positions)
    av = av.reshape(b, s, A_KV_HEADS, HEAD_DIM)
    out_a = sliding_window_attention(aq, ak, av, sinks)

    lb = lb.astype(f32).reshape(B_HEADS, B_DK)
    z = bf.reshape(b, s, B_HEADS, B_DK).astype(f32)
    log_f = jnp.logaddexp(jnp.log(lb), jnp.log1p(-lb) + jax.nn.log_sigmoid(z))
    o_b = hgrn2_chunked(bq.reshape(b, s, B_HEADS, B_DK).astype(f32), log_f,
                        bv.reshape(b, s, B_HEADS, B_DV).astype(f32))
    out_b = rmsnorm(o_b, hgrn_norm) * jax.nn.silu(bg.reshape(b, s, B_HEADS, B_DV).astype(f32))
    out_b = out_b.reshape(b, s, B_HEADS * B_DV).astype(h.dtype)

    cqkv = jax.nn.silu(causal_depthwise_conv(cqkv, conv_w))
    cq, ck, cv = jnp.split(cqkv, [C_HEADS * C_DK, 2 * C_HEADS * C_DK], axis=-1)
    cq = l2norm(cq.reshape(b, s, C_HEADS, C_DK).astype(f32)) * (C_DK ** -0.5)
    ck = l2norm(ck.reshape(b, s, C_HEADS, C_DK).astype(f32))
    cv = cv.reshape(b, s, C_HEADS, C_DV).astype(f32)
    beta = jax.nn.sigmoid(cb.astype(f32))
    g = -jnp.exp(a_log.astype(f32)) * jax.nn.softplus(ca.astype(f32) + dt_bias.astype(f32))
    o_c = gated_delta_chunked(cq, ck, cv, beta, g)
    out_c = rmsnorm(o_c, gdn_norm) * jax.nn.silu(cg.reshape(b, s, C_HEADS, C_DV).astype(f32))
    out_c = out_c.reshape(b, s, C_HEADS * C_DV).astype(h.dtype)

    mixed = jnp.concatenate([out_a, out_b, out_c], axis=-1)
    return mixed @ w_out


def swiglu(h, w_gate, w_up, w_down):
    return (jax.nn.silu(h @ w_gate) * (h @ w_up)) @ w_down


def setup_inputs(seed: int = 0) -> dict:
    key = jax.random.key(seed)
    ks = jax.random.split(key, 24)
    f32 = jnp.float32

    def nrm(k, shape, scale):
        return jax.random.normal(k, shape, f32) * scale

    x = nrm(ks[0], (BATCH, SEQ, D_MODEL), 1.0)
    c = nrm(ks[1], (BATCH, D_MODEL), 1.0)
    positions = jnp.broadcast_to(jnp.arange(SEQ, dtype=jnp.int32)[None, :], (BATCH, SEQ))
    ada_w = nrm(ks[2], (DEPTH, D_MODEL, 6 * D_MODEL), 0.5 * D_MODEL ** -0.5)
    ada_b = nrm(ks[3], (DEPTH, 6 * D_MODEL), 0.02)
    norm_mix = 1.0 + nrm(ks[4], (DEPTH, D_MODEL), 0.05)
    w_in = nrm(ks[5], (DEPTH, D_MODEL, D_IN), D_MODEL ** -0.5)
    attn_q_norm = 1.0 + nrm(ks[6], (DEPTH, HEAD_DIM), 0.05)
    attn_k_norm = 1.0 + nrm(ks[7], (DEPTH, HEAD_DIM), 0.05)
    attn_sinks = nrm(ks[8], (DEPTH, A_Q_HEADS), 1.0)
    hgrn_lb_logits = nrm(ks[9], (DEPTH, B_HEADS * B_DK), 1.0)
    hgrn_out_norm = 1.0 + nrm(ks[10], (DEPTH, B_DV), 0.05)
    gdn_conv_w = nrm(ks[11], (DEPTH, CONV_WIDTH, C_CONV_CH), CONV_WIDTH ** -0.5)
    gdn_a_log = jnp.log(jax.random.uniform(ks[12], (DEPTH, C_HEADS), f32, 1.0, 16.0))
    dt = jnp.exp(jax.random.uniform(ks[13], (DEPTH, C_HEADS), f32, math.log(1e-3), math.log(1e-1)))
    gdn_dt_bias = dt + jnp.log(-jnp.expm1(-dt))
    gdn_out_norm = 1.0 + nrm(ks[14], (DEPTH, C_DV), 0.05)
    w_out = nrm(ks[15], (DEPTH, D_MIX, D_MODEL), D_MIX ** -0.5)
    norm_ffn = 1.0 + nrm(ks[16], (DEPTH, D_MODEL), 0.05)
    w_gate = nrm(ks[17], (DEPTH, D_MODEL, D_FF), D_MODEL ** -0.5)
    w_up = nrm(ks[18], (DEPTH, D_MODEL, D_FF), D_MODEL ** -0.5)
    w_down = nrm(ks[19], (DEPTH, D_FF, D_MODEL), D_FF ** -0.5)
    return {'x': x, 'c': c, 'positions': positions, 'ada_w': ada_w, 'ada_b': ada_b,
            'norm_mix': norm_mix, 'w_in': w_in, 'attn_q_norm': attn_q_norm,
            'attn_k_norm': attn_k_norm, 'attn_sinks': attn_sinks, 'hgrn_lb_logits': hgrn_lb_logits,
            'hgrn_out_norm': hgrn_out_norm, 'gdn_conv_w': gdn_conv_w, 'gdn_a_log': gdn_a_log,
            'gdn_dt_bias': gdn_dt_bias, 'gdn_out_norm': gdn_out_norm, 'w_out': w_out,
            'norm_ffn': norm_ffn, 'w_gate': w_gate, 'w_up': w_up, 'w_down': w_down}


def reference(x, c, positions, ada_w, ada_b, norm_mix, w_in, attn_q_norm, attn_k_norm, attn_sinks,
              hgrn_lb_logits, hgrn_out_norm, gdn_conv_w, gdn_a_log, gdn_dt_bias, gdn_out_norm, w_out,
              norm_ffn, w_gate, w_up, w_down):
    lb_cum = jnp.cumsum(jax.nn.softmax(hgrn_lb_logits.astype(jnp.float32), axis=0), axis=0)
    lower_bounds = lb_cum - lb_cum[:1]
    cond = jax.nn.silu(c)
    for l in range(DEPTH):
        mod = cond @ ada_w[l] + ada_b[l]
        sh_m, sc_m, gt_m, sh_f, sc_f, gt_f = [m[:, None, :] for m in jnp.split(mod, 6, axis=-1)]
        h = rmsnorm(x, norm_mix[l]) * (1 + sc_m) + sh_m
        y = hybrid_mixer(h, positions, w_in[l], attn_q_norm[l], attn_k_norm[l], attn_sinks[l],
                         lower_bounds[l], hgrn_out_norm[l], gdn_conv_w[l], gdn_a_log[l],
                         gdn_dt_bias[l], gdn_out_norm[l], w_out[l])
        x = x + gt_m * y
        h = rmsnorm(x, norm_ffn[l]) * (1 + sc_f) + sh_f
        x = x + gt_f * swiglu(h, w_gate[l], w_up[l], w_down[l])
    return x
```

```python
import os
import numpy as np
from contextlib import ExitStack
import concourse.bass as bass
import concourse.mybir as mybir
from concourse.bass_utils import run_bass_kernel_spmd

F32 = mybir.dt.float32
BF16 = mybir.dt.bfloat16
I32 = mybir.dt.int32
AF = mybir.ActivationFunctionType
ALU = mybir.AluOpType
AX = mybir.AxisListType

S_ = 4096
D_ = 1024
DFF = 2816
DIN = 2824
NL = 2
EPS = 1e-6
NEG = -30000.0


class Buf:
    __slots__ = ("t", "w", "r")

    def __init__(self, t):
        self.t = t
        self.w = {}
        self.r = {}

    def __getitem__(self, k):
        return self.t[k]


class KB:
    EPOCH = 16000
    DMAX = 16 * 1500

    def __init__(self, nc, stack):
        self.nc = nc
        self.stack = stack
        self.eng = {"pe": nc.tensor, "act": nc.scalar, "dve": nc.vector, "pool": nc.gpsimd, "sp": nc.sync}
        self.sem = {}
        self.cnt = {}
        self.epoch = {}
        self.waited = {}
        self.nsem = 0
        for k in self.eng:
            self.epoch[k] = 0
            self.cnt[k] = 0
            self.sem[(k, 0)] = self._newsem(k + "_0")
        self.dsem = {}
        self.dcnt = {}
        self.dcur = {}
        self.dfree = []
        self.ninstr = 0

    def _newsem(self, name):
        self.nsem += 1
        return self.stack.enter_context(self.nc.semaphore("s_" + name))

    def _wait(self, k, dep):
        if dep is None:
            return
        dk, de, dn = dep
        if dk == k and k == "pe":
            return
        key = (k, dk, de)
        if self.waited.get(key, 0) >= dn:
            return
        self.waited[key] = dn
        s = self.dsem[dk] if dk in self.dsem else self.sem[(dk, de)]
        self.eng[k].wait_ge(s, dn)
        self.ninstr += 1

    def _deps(self, k, r, w):
        for b in r:
            for tok in b.w.values():
                self._wait(k, tok)
        for b in w:
            for tok in b.w.values():
                self._wait(k, tok)
            for tok in b.r.values():
                self._wait(k, tok)

    def _mark(self, tok, r, w):
        for b in r:
            b.r[(tok[0], tok[1])] = tok
        for b in w:
            b.w[(tok[0], tok[1])] = tok
            b.r = {}

    def op(self, k, fn, r=(), w=(), inc=True):
        self._deps(k, r, w)
        if inc and self.cnt[k] >= self.EPOCH:
            self.epoch[k] += 1
            self.cnt[k] = 0
            self.sem[(k, self.epoch[k])] = self._newsem("%s_%d" % (k, self.epoch[k]))
        ins = fn(self.eng[k])
        self.ninstr += 1
        if inc:
            self.cnt[k] += 1
            ins.then_inc(self.sem[(k, self.epoch[k])], 1)
            tok = (k, self.epoch[k], self.cnt[k])
        else:
            tok = (k, self.epoch[k], self.cnt[k] + 1)
        self._mark(tok, r, w)
        return tok

    def dma(self, k, stream, out, in_, r=(), w=(), **kw):
        self._deps(k, r, w)
        stream = ("L%d" % id(w[0])) if len(w) else ("S%d" % id(r[0]))
        key = self.dcur.get(stream)
        if key is None or self.dcnt[key] >= self.DMAX:
            key = None
            while self.dfree:
                cand = self.dfree.pop()
                if self.dcnt[cand] < self.DMAX // 2:
                    key = cand
                    break
            if key is None:
                key = "dq%d" % len(self.dsem)
                self.dsem[key] = self._newsem(key)
                self.dcnt[key] = 0
            self.dcur[stream] = key
        self.eng[k].dma_start(out=out, in_=in_, **kw).then_inc(self.dsem[key], 16)
        self.ninstr += 1
        self.dcnt[key] += 16
        tok = (key, 0, self.dcnt[key])
        self._mark(tok, r, w)
        return tok

    def phase_end(self):
        self.barrier()
        for key in self.dcur.values():
            self.dfree.append(key)
        self.dcur = {}

    def barrier(self):
        toks = []
        for k in self.eng:
            for ep in range(self.epoch[k] + 1):
                n = self.cnt[k] if ep == self.epoch[k] else self.EPOCH
                if n > 0:
                    toks.append((k, ep, n))
        for s in list(self.dsem.keys()):
            if self.dcnt[s] > 0:
                toks.append((s, 0, self.dcnt[s]))
        for k in self.eng:
            for t in toks:
                if t[0] != k:
                    self._wait(k, t)


def host_consts():
    c = {}
    c["ident"] = np.eye(128, dtype=np.float32)
    rr = np.zeros((64, 64), np.float32)
    for i in range(8):
        rr[i + 8, i] = -1.0
        rr[i, i + 8] = 1.0
    c["rrot"] = np.pad(rr, ((0, 64), (0, 0)))
    inv_freq = (500000.0 ** (-np.arange(8, dtype=np.float32) * 2.0 / 16.0)).astype(np.float32)
    f = np.zeros((128, 1), np.float32)
    for d in range(16):
        f[d, 0] = inv_freq[d % 8]
    c["invf"] = f
    k = np.arange(128)[:, None]
    q = np.arange(128)[None, :]
    c["mcur"] = np.tile((k <= q).astype(np.float32), (1, 4))
    c["mprev"] = np.tile((k > q).astype(np.float32), (1, 4))
    u = np.arange(64)[:, None]
    t = np.arange(64)[None, :]
    z = lambda a: np.pad(a.astype(np.float32), ((0, 64), (0, 0)))
    c["mcum"] = z(u <= t)
    c["mmid"] = z((u <= t).astype(np.float32) - (u <= 31).astype(np.float32))
    c["mlc"] = z(u > t)
    c["mones"] = z(np.ones((64, 64)))
    c["caus"] = z(np.tile((u <= t).astype(np.float32), (1, 4)))
    cap = np.zeros((64, 4, 2, 64), np.float32)
    cap[:, :, 0, :] = np.where(u <= t, 0.0, NEG)[:, None, :]
    cap[:, :, 1, :] = np.where(u < t, 0.0, NEG)[:, None, :]
    c["cap"] = z(cap.reshape(64, 512))
    off = {}
    o = 0
    arrs = []
    for n, a in c.items():
        off[n] = (o, a.shape[1])
        o += a.shape[1]
        arrs.append(a)
    return np.ascontiguousarray(np.concatenate(arrs, axis=1)), off


CST, COFF = host_consts()
NCST = CST.shape[1]


def build(dbg=(), nlayers=NL, phases=None):
    nc = bass.Bass("TRN2", target_bir_lowering=False)

    def dram(name, shape, dt, kind="ExternalInput"):
        return nc.dram_tensor(name, shape, dt, kind=kind).ap()

    def scratch(name, shape, dt):
        return dram(name, shape, dt, "ExternalOutput" if name in dbg else "Internal")

    x_d = dram("x", [S_, D_], F32)
    cT_d = dram("cT", [128, 8], F32)
    pos_d = dram("pos", [1, S_], I32)
    adaw_d = dram("ada_w", [NL, D_, 6 * D_], F32)
    adab_d = dram("ada_bT", [NL, 128, 48], F32)
    nmix_d = dram("nmixT", [NL, 128, 8], F32)
    nffn_d = dram("nffnT", [NL, 128, 8], F32)
    win_d = dram("w_in", [NL, D_, DIN], F32)
    wout_d = dram("w_out", [NL, D_, D_], F32)
    wg_d = dram("w_gate", [NL, D_, DFF], F32)
    wu_d = dram("w_up", [NL, D_, DFF], F32)
    wd_d = dram("w_down", [NL, DFF, D_], F32)
    sm_d = dram("small", [NL, 1, 1024], F32)
    cw_d = dram("convT", [NL, 128, 26], F32)
    cst_d = dram("cst", [128, NCST], F32)
    out_d = dram("out", [S_, D_], F32, "ExternalOutput")

    xT_d = scratch("xT", [D_, S_], F32)
    projT_d = scratch("projT", [1408, S_], F32)
    projK_d = scratch("projK", [S_, 1416], F32)
    convK_d = scratch("convK", [S_, 768], F32)
    mixed_d = scratch("mixed", [S_, D_], BF16)

    def run(p):
        return phases is None or p in phases

    with ExitStack() as top:
        K = KB(nc, top)
        K.allbufs = []

        uid = [0]

        def SB(st, name, shape, dt):
            uid[0] += 1
            b = Buf(st.enter_context(nc.sbuf_tensor("%s_%d" % (name, uid[0]), shape, dt)))
            K.allbufs.append(b)
            return b

        def PS(st, name, shape, dt):
            uid[0] += 1
            b = Buf(st.enter_context(nc.psum_tensor("%s_%d" % (name, uid[0]), shape, dt)))
            K.allbufs.append(b)
            return b

        cst = SB(top, "cst_sb", [128, NCST], F32)
        cstb = SB(top, "cstb", [128, NCST], BF16)
        onesb = SB(top, "onesb", [128, 128], BF16)
        modt = SB(top, "modt", [128, 48], F32)
        gscm = SB(top, "gscm", [128, 8], F32)
        gscf = SB(top, "gscf", [128, 8], F32)
        epsb = SB(top, "epsb", [128, 1], F32)
        npib = SB(top, "npib", [128, 1], F32)
        K.dma("sp", "ldc", cst[:, :], cst_d[:, :], w=[cst])
        K.op("dve", lambda e: e.tensor_copy(out=cstb[:, :], in_=cst[:, :]), r=[cst], w=[cstb])
        K.op("dve", lambda e: e.memset(onesb[:, :], 1.0), w=[onesb])
        K.op("dve", lambda e: e.memset(epsb[:, :], EPS), w=[epsb])
        K.op("dve", lambda e: e.memset(npib[:, :], -float(np.pi)), w=[npib])

        def C(name, rows=128, bf=False):
            o, n = COFF[name]
            return (cstb if bf else cst)[0:rows, o:o + n]

        CB = cstb
        CF = cst

        if run("p0"):
            with ExitStack() as st:
                xin = [SB(st, "p0x%d" % i, [128, D_], F32) for i in range(2)]
                xo = [SB(st, "p0o%d" % i, [128, 8, 128], F32) for i in range(2)]
                pt = [PS(st, "p0p%d" % i, [128, 8, 128], F32) for i in range(2)]
                for i in range(S_ // 128):
                    a = xin[i % 2]
                    o = xo[i % 2]
                    p = pt[i % 2]
                    K.dma("sp", "ld", a[:, :], x_d[i * 128:(i + 1) * 128, :], w=[a])
                    for c in range(8):
                        K.op("pe", lambda e: e.transpose(p[:, c, :], a[:, c * 128:(c + 1) * 128], CF[:, COFF["ident"][0]:COFF["ident"][0] + 128]),
                             r=[a, cst], w=[p])
                    K.op("act" if i % 2 else "dve", (lambda e: e.copy(out=o[:, :, :], in_=p[:, :, :])) if i % 2 else
                         (lambda e: e.tensor_copy(out=o[:, :, :], in_=p[:, :, :])), r=[p], w=[o])
                    K.dma("pool", "st", xT_d.rearrange("(c p) t -> p c t", p=128)[:, :, i * 128:(i + 1) * 128], o[:, :, :], r=[o])
            K.phase_end()

        for l in range(nlayers):
            if run("ada"):
                with ExitStack() as st:
                    cnd = SB(st, "a_c", [128, 8], F32)
                    tmp = SB(st, "a_t", [128, 8], F32)
                    ab = SB(st, "a_b", [128, 48], F32)
                    nm = SB(st, "a_nm", [128, 16], F32)
                    wp = [SB(st, "a_w%d" % i, [128, 8, 1024], F32) for i in range(2)]
                    mp = PS(st, "a_ps", [128, 48], F32)
                    K.dma("sp", "ld", cnd[:, :], cT_d[:, :], w=[cnd])
                    K.dma("sp", "ld", ab[:, :], adab_d[l, :, :], w=[ab])
                    K.dma("sp", "ld", nm[:, 0:8], nmix_d[l, :, :], w=[nm])
                    K.dma("sp", "ld", nm[:, 8:16], nffn_d[l, :, :], w=[nm])
                    K.op("act", lambda e: e.activation(out=tmp[:, :], in_=cnd[:, :], func=AF.Exp, scale=-1.0), r=[cnd], w=[tmp])
                    K.op("dve", lambda e: e.tensor_scalar(out=tmp[:, :], in0=tmp[:, :], scalar1=1.0, scalar2=None, op0=ALU.add), r=[tmp], w=[tmp])
                    K.op("dve", lambda e: e.reciprocal(out=tmp[:, :], in_=tmp[:, :]), r=[tmp], w=[tmp])
                    K.op("dve", lambda e: e.tensor_tensor(out=cnd[:, :], in0=cnd[:, :], in1=tmp[:, :], op=ALU.mult), r=[cnd, tmp], w=[cnd])
                    for g in range(6):
                        w = wp[g % 2]
                        for kc in range(8):
                            K.dma("sp", "ld", w[:, kc, :], adaw_d[l, kc * 128:(kc + 1) * 128, g * 1024:(g + 1) * 1024], w=[w])
                        for oc in range(8):
                            for kc in range(8):
                                K.op("pe", lambda e: e.matmul(mp[:, g * 8 + oc:g * 8 + oc + 1], lhsT=w[:, kc, oc * 128:(oc + 1) * 128], rhs=cnd[:, kc:kc + 1],
                                                              start=(kc == 0), stop=(kc == 7)), r=[w, cnd], w=[mp])
                    K.op("dve", lambda e: e.tensor_tensor(out=modt[:, :], in0=mp[:, :], in1=ab[:, :], op=ALU.add), r=[mp, ab], w=[modt])
                    K.op("dve", lambda e: e.scalar_tensor_tensor(out=gscm[:, :], in0=modt[:, 8:16], scalar=1.0, in1=nm[:, 0:8], op0=ALU.add, op1=ALU.mult),
                         r=[modt, nm], w=[gscm])
                    K.op("dve", lambda e: e.scalar_tensor_tensor(out=gscf[:, :], in0=modt[:, 32:40], scalar=1.0, in1=nm[:, 8:16], op0=ALU.add, op1=ALU.mult),
                         r=[modt, nm], w=[gscf])
                K.phase_end()

            def load_weight_bf16(st, name, src_rows, nkc, ncols, dst, piece):
                stg = [SB(st, name + "_s%d" % i, [128, piece], F32) for i in range(3)]
                i = 0
                for kc in range(nkc):
                    for c0 in range(0, ncols, piece):
                        n = min(piece, ncols - c0)
                        s = stg[i % 3]
                        K.dma("sp", "ldw", s[:, 0:n], src_rows(kc)[:, c0:c0 + n], w=[s])
                        eng = ("dve", "pool", "act")[i % 3]
                        if eng == "act":
                            K.op("act", lambda e: e.copy(out=dst[:, kc, c0:c0 + n], in_=s[:, 0:n]), r=[s], w=[dst])
                        else:
                            K.op(eng, lambda e: e.tensor_copy(out=dst[:, kc, c0:c0 + n], in_=s[:, 0:n]), r=[s], w=[dst])
                        i += 1

            def rms_h(st, xt, hT, gsc, shcol, pfx, sq, rstd, tmpf, pss):
                K.op("act", lambda e: e.activation(out=sq[:, :, :], in_=xt[:, :, :], func=AF.Square), r=[xt], w=[sq])
                for c in range(8):
                    K.op("pe", lambda e: e.matmul(pss[:, :], lhsT=onesb[:, :], rhs=sq[:, c, :], start=(c == 0), stop=(c == 7)), r=[onesb, sq], w=[pss])
                K.op("act", lambda e: e.activation(out=rstd[:, :], in_=pss[:, :], func=AF.Sqrt, bias=epsb[:, :], scale=1.0 / D_), r=[pss, epsb], w=[rstd])
                K.op("dve", lambda e: e.reciprocal(out=rstd[:, :], in_=rstd[:, :]), r=[rstd], w=[rstd])
                for c in range(8):
                    t = tmpf[c % 2]
                    K.op("dve", lambda e: e.tensor_tensor(out=t[:, :], in0=xt[:, c, :], in1=rstd[:, :], op=ALU.mult), r=[xt, rstd], w=[t])
                    K.op("act", lambda e: e.activation(out=hT[:, c, :], in_=t[:, :], func=AF.Identity, bias=modt[:, shcol + c:shcol + c + 1],
                                                       scale=gsc[:, c:c + 1]), r=[t, modt, gsc], w=[hT])

            xT_v = xT_d.rearrange("(c p) t -> p c t", p=128)

            if run("m1"):
                with ExitStack() as st:
                    win = SB(st, "m1w", [128, 8, DIN], BF16)
                    load_weight_bf16(st, "m1w", lambda kc: win_d[l, kc * 128:(kc + 1) * 128, :], 8, DIN, win, 706)
                    xt = [SB(st, "m1x%d" % i, [128, 8, 512], F32) for i in range(2)]
                    hT = SB(st, "m1h", [128, 8, 512], BF16)
                    sq = SB(st, "m1sq", [128, 8, 512], BF16)
                    rstd = SB(st, "m1r", [128, 512], F32)
                    tmpf = [SB(st, "m1t%d" % i, [128, 512], F32) for i in range(2)]
                    pss = PS(st, "m1pss", [128, 512], F32)
                    pp = [PS(st, "m1pp%d" % i, [128, 512], F32) for i in range(4)]
                    og = [SB(st, "m1o%d" % i, [128, 512], F32) for i in range(4)]
                    fchunks = [c * 128 for c in range(5)] + [1792 + c * 128 for c in range(6)]
                    kgroups = [(640, 512, 0), (1152, 512, 512), (1664, 128, 1024), (2560, 264, 1152)]
                    n = 0
                    for ti in range(S_ // 512):
                        x_ = xt[ti % 2]
                        K.dma("sp", "ld", x_[:, :, :], xT_v[:, :, ti * 512:(ti + 1) * 512], w=[x_])
                        rms_h(st, x_, hT, gscm, 0, "m1", sq, rstd, tmpf, pss)
                        for fi, c0 in enumerate(fchunks):
                            p = pp[n % 4]
                            o = og[n % 4]
                            for kc in range(8):
                                K.op("pe", lambda e: e.matmul(p[:, :], lhsT=win[:, kc, c0:c0 + 128], rhs=hT[:, kc, :], start=(kc == 0), stop=(kc == 7)),
                                     r=[win, hT], w=[p], inc=(kc == 7))
                            if n % 2:
                                K.op("act", lambda e: e.copy(out=o[:, :], in_=p[:, :]), r=[p], w=[o])
                            else:
                                K.op("dve", lambda e: e.tensor_copy(out=o[:, :], in_=p[:, :]), r=[p], w=[o])
                            K.dma("pool", "st", projT_d[fi * 128:(fi + 1) * 128, ti * 512:(ti + 1) * 512], o[:, :], r=[o])
                            n += 1
                        for sub in range(4):
                            for (c0, nn, d0) in kgroups:
                                p = pp[n % 4]
                                o = og[n % 4]
                                for kc in range(8):
                                    K.op("pe", lambda e: e.matmul(p[:, 0:nn], lhsT=hT[:, kc, sub * 128:(sub + 1) * 128], rhs=win[:, kc, c0:c0 + nn],
                                                                  start=(kc == 0), stop=(kc == 7)), r=[win, hT], w=[p], inc=(kc == 7))
                                if n % 2:
                                    K.op("act", lambda e: e.copy(out=o[:, 0:nn], in_=p[:, 0:nn]), r=[p], w=[o])
                                else:
                                    K.op("dve", lambda e: e.tensor_copy(out=o[:, 0:nn], in_=p[:, 0:nn]), r=[p], w=[o])
                                r0 = ti * 512 + sub * 128
                                K.dma("pool", "st", projK_d[r0:r0 + 128, d0:d0 + nn], o[:, 0:nn], r=[o])
                                n += 1
                K.phase_end()

            sm = sm_d[l]

            if run("m2"):
                with ExitStack() as st:
                    cosF = SB(st, "cosF", [64, S_], F32)
                    sinF = SB(st, "sinF", [64, S_], F32)
                    st0 = st
                    pi_ = SB(st0, "rp_i", [64, S_], I32)
                    u = SB(st0, "rp_u", [64, S_], F32)
                    ki = SB(st0, "rp_k", [64, S_], I32)
                    kf = SB(st0, "rp_kf", [64, S_], F32)
                    m = SB(st0, "rp_m", [64, S_], F32)
                    K.dma("sp", "ld", pi_[:, :], pos_d[0:1, :].to_broadcast([64, S_]), w=[pi_])
                    K.op("dve", lambda e: e.tensor_copy(out=u[:, :], in_=pi_[:, :]), r=[pi_], w=[u])
                    o_if = COFF["invf"][0]
                    K.op("dve", lambda e: e.tensor_scalar(out=u[:, :], in0=u[:, :], scalar1=CF[0:64, o_if:o_if + 1], scalar2=float(1.0 / (2 * np.pi)),
                                                          op0=ALU.mult, op1=ALU.mult), r=[u, cst], w=[u])
                    for (dst, shift) in ((sinF, 0.5), (cosF, 0.75)):
                        K.op("dve", lambda e: e.tensor_scalar(out=m[:, :], in0=u[:, :], scalar1=float(shift), scalar2=None, op0=ALU.add), r=[u], w=[m])
                        K.op("dve", lambda e: e.tensor_copy(out=ki[:, :], in_=m[:, :]), r=[m], w=[ki])
                        K.op("dve", lambda e: e.tensor_copy(out=kf[:, :], in_=ki[:, :]), r=[ki], w=[kf])
                        K.op("dve", lambda e: e.tensor_tensor(out=m[:, :], in0=m[:, :], in1=kf[:, :], op=ALU.subtract), r=[m, kf], w=[m])
                        K.op("dve", lambda e: e.tensor_scalar(out=kf[:, :], in0=m[:, :], scalar1=0.0, scalar2=None, op0=ALU.is_lt), r=[m], w=[kf])
                        K.op("dve", lambda e: e.tensor_tensor(out=m[:, :], in0=m[:, :], in1=kf[:, :], op=ALU.add), r=[m, kf], w=[m])
                        K.op("dve", lambda e: e.tensor_scalar(out=kf[:, :], in0=m[:, :], scalar1=1.0, scalar2=None, op0=ALU.is_ge), r=[m], w=[kf])
                        K.op("dve", lambda e: e.tensor_tensor(out=m[:, :], in0=m[:, :], in1=kf[:, :], op=ALU.subtract), r=[m, kf], w=[m])
                        K.op("act", lambda e: e.activation(out=dst[:, :], in_=m[:, :], func=AF.Sin, bias=npib[0:64, :], scale=float(2 * np.pi)),
                             r=[m, npib], w=[dst])
                    gq = SB(st, "m2g", [64, 2], F32)
                    es = SB(st, "m2es", [128, 8], F32)
                    K.dma("sp", "ld", gq[:, 0:2], cw_d[l, 0:64, 24:26], w=[gq])
                    K.dma("sp", "ld", es[:, :], sm[0:1, 128:136].to_broadcast([128, 8]), w=[es])
                    K.op("act", lambda e: e.activation(out=es[:, :], in_=es[:, :], func=AF.Exp), r=[es], w=[es])
                    raw = [SB(st, "m2raw%d" % i, [64, 10, 128], F32) for i in range(2)]
                    sqb = SB(st, "m2sq", [64, 10, 128], BF16)
                    rs = SB(st, "m2rs", [64, 10, 128], F32)
                    qn = SB(st, "m2qn", [64, 10, 128], F32)
                    qnb = SB(st, "m2qnb", [64, 10, 128], BF16)
                    t1 = SB(st, "m2t1", [64, 10, 128], F32)
                    t2 = SB(st, "m2t2", [64, 10, 128], F32)
                    qk = [SB(st, "m2qk%d" % i, [64, 10, 128], BF16) for i in range(2)]
                    vraw = [SB(st, "m2vr%d" % i, [128, 128], F32) for i in range(2)]
                    vb = [SB(st, "m2vb%d" % i, [128, 2, 65], BF16) for i in range(2)]
                    pT = [SB(st, "m2pT%d" % i, [128, 512], BF16) for i in range(4)]
                    den = SB(st, "m2den", [128, 8], F32)
                    oo = [SB(st, "m2oo%d" % i, [128, 8, 64], BF16) for i in range(2)]
                    ps3 = PS(st, "m2ps3", [64, 3, 512], F32)
                    psS = [PS(st, "m2pS%d" % i, [128, 512], F32) for i in range(2)]
                    psO = PS(st, "m2pO", [128, 8, 128], F32)
                    for b in vb:
                        K.op("dve", lambda e: e.memset(b[:, :, 64:65], 1.0), w=[b])
                    o_rr = COFF["rrot"][0]
                    o_mc = COFF["mcur"][0]
                    o_mp = COFF["mprev"][0]
                    nb = S_ // 128
                    for i in range(nb):
                        rw = raw[i % 2]
                        cur = qk[i % 2]
                        prv = qk[(i + 1) % 2]
                        K.dma("sp", "ld", rw[:, 0:8, :], projT_d[0:512, i * 128:(i + 1) * 128].rearrange("(h d) t -> d h t", d=64), w=[rw])
                        K.dma("sp", "ld", rw[:, 8:10, :], projT_d[512:640, i * 128:(i + 1) * 128].rearrange("(h d) t -> d h t", d=64), w=[rw])
                        vr = vraw[i % 2]
                        K.dma("sp", "ld", vr[:, :], projK_d[i * 128:(i + 1) * 128, 0:128], w=[vr])
                        vcur = vb[i % 2]
                        vprv = vb[(i + 1) % 2]
                        K.op("pool", lambda e: e.tensor_copy(out=vcur[:, :, 0:64], in_=vr[:, :].rearrange("p (g d) -> p g d", d=64)), r=[vr], w=[vcur])
                        K.op("act", lambda e: e.activation(out=sqb[:, :, :], in_=rw[:, :, :], func=AF.Square), r=[rw], w=[sqb])
                        for j, (a, b_) in enumerate(((0, 4), (4, 8), (8, 10))):
                            K.op("pe", lambda e: e.matmul(ps3[:, j, 0:(b_ - a) * 128], lhsT=onesb[0:64, 0:64], rhs=sqb[:, a:b_, :], start=True, stop=True),
                                 r=[onesb, sqb], w=[ps3])
                        K.op("act", lambda e: e.activation(out=rs[:, :, :].rearrange("p h t -> p (h t)"), in_=ps3[:, :, :].rearrange("p a n -> p (a n)")[:, 0:1280],
                                                           func=AF.Ln, bias=epsb[0:64, :], scale=1.0 / 64), r=[ps3, epsb], w=[rs])
                        K.op("act", lambda e: e.activation(out=rs[:, :, :], in_=rs[:, :, :], func=AF.Exp, scale=-0.5), r=[rs], w=[rs])
                        K.op("dve", lambda e: e.scalar_tensor_tensor(out=qn[:, 0:8, :], in0=rw[:, 0:8, :], scalar=gq[:, 0:1], in1=rs[:, 0:8, :], op0=ALU.mult, op1=ALU.mult),
                             r=[rw, gq, rs], w=[qn])
                        K.op("dve", lambda e: e.scalar_tensor_tensor(out=qn[:, 8:10, :], in0=rw[:, 8:10, :], scalar=gq[:, 1:2], in1=rs[:, 8:10, :], op0=ALU.mult, op1=ALU.mult),
                             r=[rw, gq, rs], w=[qn])
                        K.op("act", lambda e: e.copy(out=qnb[:, :, :], in_=qn[:, :, :]), r=[qn], w=[qnb])
                        for j, (a, b_) in enumerate(((0, 4), (4, 8), (8, 10))):
                            K.op("pe", lambda e: e.matmul(ps3[:, j, 0:(b_ - a) * 128], lhsT=CB[0:64, o_rr:o_rr + 64], rhs=qnb[:, a:b_, :], start=True, stop=True),
                                 r=[cstb, qnb], w=[ps3])
                        cs = cosF[:, i * 128:(i + 1) * 128]
                        sn = sinF[:, i * 128:(i + 1) * 128]
                        K.op("dve", lambda e: e.tensor_tensor(out=t1[:, :, :], in0=qn[:, :, :], in1=cs.rearrange("p (o t) -> p o t", o=1).to_broadcast([64, 10, 128]), op=ALU.mult),
                             r=[qn, cosF], w=[t1])
                        K.op("dve", lambda e: e.tensor_tensor(out=t2[:, :, :], in0=ps3[:, :, :].rearrange("p a n -> p (a n)")[:, 0:1280].rearrange("p (h t) -> p h t", t=128),
                                                              in1=sn.rearrange("p (o t) -> p o t", o=1).to_broadcast([64, 10, 128]), op=ALU.mult), r=[ps3, sinF], w=[t2])
                        K.op("dve", lambda e: e.tensor_tensor(out=cur[:, :, :], in0=t1[:, :, :], in1=t2[:, :, :], op=ALU.add), r=[t1, t2], w=[cur])
                        for g in range(2):
                            pc = pT[2 * g]
                            pp_ = pT[2 * g + 1]
                            K.op("pe", lambda e: e.matmul(psS[0][:, :], lhsT=cur[:, 8 + g, :], rhs=cur[:, 4 * g:4 * g + 4, :], start=True, stop=True), r=[cur], w=[psS[0]])
                            K.op("act", lambda e: e.activation(out=pc[:, :], in_=psS[0][:, :], func=AF.Exp, scale=0.125), r=[psS[0]], w=[pc])
                            K.op("dve", lambda e: e.tensor_tensor(out=pc[:, :], in0=pc[:, :], in1=CB[:, o_mc:o_mc + 512], op=ALU.mult), r=[pc, cstb], w=[pc])
                            if i > 0:
                                K.op("pe", lambda e: e.matmul(psS[1][:, :], lhsT=prv[:, 8 + g, :], rhs=cur[:, 4 * g:4 * g + 4, :], start=True, stop=True), r=[cur, prv], w=[psS[1]])
                                K.op("act", lambda e: e.activation(out=pp_[:, :], in_=psS[1][:, :], func=AF.Exp, scale=0.125), r=[psS[1]], w=[pp_])
                                K.op("pool", lambda e: e.tensor_tensor(out=pp_[:, :], in0=pp_[:, :], in1=CB[:, o_mp:o_mp + 512], op=ALU.mult), r=[pp_, cstb], w=[pp_])
                            for j in range(4):
                                h = 4 * g + j
                                K.op("pe", lambda e: e.matmul(psO[:, h, 0:65], lhsT=pc[:, j * 128:(j + 1) * 128], rhs=vcur[:, g, :], start=True, stop=(i == 0)),
                                     r=[pc, vcur], w=[psO])
                                if i > 0:
                                    K.op("pe", lambda e: e.matmul(psO[:, h, 0:65], lhsT=pp_[:, j * 128:(j + 1) * 128], rhs=vprv[:, g, :], start=False, stop=True),
                                         r=[pp_, vprv], w=[psO])
                        K.op("dve", lambda e: e.tensor_tensor(out=den[:, :], in0=psO[:, :, 64], in1=es[:, :], op=ALU.add), r=[psO, es], w=[den])
                        K.op("dve", lambda e: e.reciprocal(out=den[:, :], in_=den[:, :]), r=[den], w=[den])
                        o_ = oo[i % 2]
                        K.op("dve", lambda e: e.tensor_tensor(out=o_[:, :, :], in0=psO[:, :, 0:64], in1=den[:, :].rearrange("p (h o) -> p h o", o=1).to_broadcast([128, 8, 64]),
                                                              op=ALU.mult), r=[psO, den], w=[o_])
                        K.dma("pool", "st", mixed_d[i * 128:(i + 1) * 128, 0:512], o_[:, :, :].rearrange("p h d -> p (h d)"), r=[o_])
                K.phase_end()

            def out_norm_gate(o_ps, gate_src, gnb, dst_cols, r0, tmpA, tmpB, ssq, ob, extra_r):
                K.op("act", lambda e: e.activation(out=tmpA[:, :].rearrange("p (h d) -> p h d", d=64), in_=o_ps[:, 0:4, :], func=AF.Square), r=[o_ps], w=[tmpA])
                K.op("dve", lambda e: e.tensor_reduce(out=ssq[:, :], in_=tmpA[:, :].rearrange("p (h d) -> p h d", d=64), axis=AX.X, op=ALU.add), r=[tmpA], w=[ssq])
                K.op("act", lambda e: e.activation(out=ssq[:, :], in_=ssq[:, :], func=AF.Ln, bias=epsb[0:64, :], scale=1.0 / 64), r=[ssq, epsb], w=[ssq])
                K.op("act", lambda e: e.activation(out=ssq[:, :], in_=ssq[:, :], func=AF.Exp, scale=-0.5), r=[ssq], w=[ssq])
                K.op("dve", lambda e: e.tensor_tensor(out=tmpA[:, :].rearrange("p (h d) -> p h d", d=64), in0=o_ps[:, 0:4, :],
                                                      in1=ssq[:, :].rearrange("p (h o) -> p h o", o=1).to_broadcast([64, 4, 64]), op=ALU.mult), r=[o_ps, ssq], w=[tmpA])
                K.op("dve", lambda e: e.tensor_tensor(out=tmpA[:, :], in0=tmpA[:, :], in1=gnb[:, :], op=ALU.mult), r=[tmpA, gnb], w=[tmpA])
                K.op("act", lambda e: e.activation(out=tmpB[:, :], in_=gate_src, func=AF.Exp, scale=-1.0), r=extra_r, w=[tmpB])
                K.op("dve", lambda e: e.tensor_scalar(out=tmpB[:, :], in0=tmpB[:, :], scalar1=1.0, scalar2=None, op0=ALU.add), r=[tmpB], w=[tmpB])
                K.op("dve", lambda e: e.reciprocal(out=tmpB[:, :], in_=tmpB[:, :]), r=[tmpB], w=[tmpB])
                K.op("dve", lambda e: e.tensor_tensor(out=tmpB[:, :], in0=tmpB[:, :], in1=gate_src, op=ALU.mult), r=[tmpB] + list(extra_r), w=[tmpB])
                K.op("dve", lambda e: e.tensor_tensor(out=ob[:, :], in0=tmpA[:, :], in1=tmpB[:, :], op=ALU.mult), r=[tmpA, tmpB], w=[ob])
                K.dma("pool", "st", mixed_d[r0:r0 + 64, dst_cols[0]:dst_cols[1]], ob[:, :], r=[ob])

            o_id = COFF["ident"][0]
            o_cum = COFF["mcum"][0]
            o_mid = COFF["mmid"][0]
            o_lc = COFF["mlc"][0]
            o_on = COFF["mones"][0]
            o_ca = COFF["caus"][0]
            o_cap = COFF["cap"][0]

            if run("m3"):
                NB3 = 2
                with ExitStack() as st:
                    lbb = SB(st, "m3lb", [64, 256], F32)
                    oml = SB(st, "m3oml", [64, 256], F32)
                    gnb = SB(st, "m3gn", [64, 256], F32)
                    if l == 0:
                        K.op("dve", lambda e: e.memset(lbb[:, :], 0.0), w=[lbb])
                    else:
                        t0 = SB(st, "m3lt", [64, 256], F32)
                        K.dma("sp", "ld", lbb[:, :], sm_d[0, 0:1, 136:392].to_broadcast([64, 256]), w=[lbb])
                        K.dma("sp", "ld", t0[:, :], sm_d[1, 0:1, 136:392].to_broadcast([64, 256]), w=[t0])
                        K.op("dve", lambda e: e.tensor_tensor(out=lbb[:, :], in0=lbb[:, :], in1=t0[:, :], op=ALU.subtract), r=[lbb, t0], w=[lbb])
                        K.op("act", lambda e: e.activation(out=lbb[:, :], in_=lbb[:, :], func=AF.Exp), r=[lbb], w=[lbb])
                        K.op("dve", lambda e: e.tensor_scalar(out=lbb[:, :], in0=lbb[:, :], scalar1=1.0, scalar2=None, op0=ALU.add), r=[lbb], w=[lbb])
                        K.op("dve", lambda e: e.reciprocal(out=lbb[:, :], in_=lbb[:, :]), r=[lbb], w=[lbb])
                    K.op("dve", lambda e: e.tensor_scalar(out=oml[:, :], in0=lbb[:, :], scalar1=-1.0, scalar2=1.0, op0=ALU.mult, op1=ALU.add), r=[lbb], w=[oml])
                    for h in range(4):
                        K.dma("sp", "ld", gnb[:, h * 64:(h + 1) * 64], sm[0:1, 648:712].to_broadcast([64, 64]), w=[gnb])
                    NB = NB3
                    inp = [SB(st, "m3in%d" % i, [64, NB, 1024], F32) for i in range(2)]
                    e_ = SB(st, "m3e", [64, NB, 256], F32)
                    s_ = SB(st, "m3s", [64, NB, 256], F32)
                    f_ = SB(st, "m3f", [64, NB, 256], F32)
                    lf = SB(st, "m3lf", [64, NB, 256], F32)
                    kk = SB(st, "m3kk", [64, NB, 256], F32)
                    ex = SB(st, "m3ex", [64, NB, 4, 256], F32)
                    qq = SB(st, "m3qq", [64, NB, 4, 256], F32)
                    trs = SB(st, "m3tr", [64, NB, 12, 64], F32)
                    att = SB(st, "m3att", [64, NB, 256], F32)
                    el = SB(st, "m3el", [64, NB, 4], F32)
                    Sf = SB(st, "m3S", [64, 4, 64], F32)
                    tA = SB(st, "m3tA", [64, NB, 256], F32)
                    tB = SB(st, "m3tB", [64, NB, 256], F32)
                    ssq = SB(st, "m3ssq", [64, NB, 4], F32)
                    ob = [SB(st, "m3ob%d" % i, [64, NB, 256], BF16) for i in range(2)]
                    pc = PS(st, "m3pc", [64, NB, 512], F32)
                    pB = PS(st, "m3pB", [64, 512], F32)
                    ptr = PS(st, "m3ptr", [64, 3, 512], F32)
                    pD_ = PS(st, "m3pD", [64, 512], F32)
                    pE = PS(st, "m3pE", [64, 512], F32)
                    K.op("dve", lambda e: e.memset(Sf[:, :, :], 0.0), w=[Sf])
                    K.op("dve", lambda e: e.memset(att[:, :, :], 0.0), w=[att])
                    identf3 = CF[0:64, o_id:o_id + 64]
                    ptrf = ptr[:, :, :].rearrange("p b x -> p (b x)")
                    msk3 = CF[0:64, o_ca:o_ca + 256].rearrange("p (h t) -> p h t", t=64)
                    bcn = lambda t2: t2.rearrange("p (o x) -> p o x", o=1).to_broadcast([64, NB, 256])
                    for gi_ in range(S_ // (64 * NB)):
                        r0 = gi_ * 64 * NB
                        x_ = inp[gi_ % 2]
                        K.dma("sp", "ld", x_[:, :, :], projK_d[r0:r0 + 64 * NB, 128:1152].rearrange("(n p) c -> p n c", p=64), w=[x_])
                        K.op("act", lambda e: e.activation(out=e_[:, :, :], in_=x_[:, :, 256:512], func=AF.Exp, scale=-1.0), r=[x_], w=[e_])
                        K.op("dve", lambda e: e.tensor_scalar(out=s_[:, :, :], in0=e_[:, :, :], scalar1=1.0, scalar2=None, op0=ALU.add), r=[e_], w=[s_])
                        K.op("dve", lambda e: e.reciprocal(out=s_[:, :, :], in_=s_[:, :, :]), r=[s_], w=[s_])
                        K.op("dve", lambda e: e.tensor_tensor(out=s_[:, :, :], in0=s_[:, :, :], in1=bcn(oml[:, :]), op=ALU.mult), r=[s_, oml], w=[s_])
                        K.op("dve", lambda e: e.tensor_tensor(out=f_[:, :, :], in0=s_[:, :, :], in1=bcn(lbb[:, :]), op=ALU.add), r=[s_, lbb], w=[f_])
                        K.op("act", lambda e: e.activation(out=lf[:, :, :], in_=f_[:, :, :], func=AF.Ln), r=[f_], w=[lf])
                        K.op("dve", lambda e: e.tensor_tensor(out=kk[:, :, :], in0=s_[:, :, :], in1=e_[:, :, :], op=ALU.mult), r=[s_, e_], w=[kk])
                        for n in range(NB):
                            K.op("pe", lambda e: e.matmul(pc[:, n, 0:256], lhsT=CF[0:64, o_cum:o_cum + 64], rhs=lf[:, n, :], start=True, stop=True), r=[cst, lf], w=[pc])
                            K.op("pe", lambda e: e.matmul(pc[:, n, 256:512], lhsT=CF[0:64, o_mid:o_mid + 64], rhs=lf[:, n, :], start=True, stop=True), r=[cst, lf], w=[pc])
                            K.op("pe", lambda e: e.matmul(pB[:, n * 256:(n + 1) * 256], lhsT=CF[0:64, o_lc:o_lc + 64], rhs=lf[:, n, :], start=True, stop=True), r=[cst, lf], w=[pB])
                            for h in range(4):
                                K.op("pe", lambda e: e.matmul(pD_[:, n * 4 + h:n * 4 + h + 1], lhsT=lf[:, n, h * 64:(h + 1) * 64], rhs=CF[0:64, o_on:o_on + 1], start=True, stop=True),
                                     r=[cst, lf], w=[pD_])
                        K.op("act", lambda e: e.activation(out=ex[:, :, 0:2, :].rearrange("p n a x -> p n (a x)"), in_=pc[:, :, :], func=AF.Exp), r=[pc], w=[ex])
                        K.op("act", lambda e: e.activation(out=ex[:, :, 2, :], in_=pc[:, :, 256:512], func=AF.Exp, scale=-1.0), r=[pc], w=[ex])
                        K.op("act", lambda e: e.activation(out=ex[:, :, 3, :], in_=pB[:, :].rearrange("p (n x) -> p n x", n=NB), func=AF.Exp), r=[pB], w=[ex])
                        K.op("act", lambda e: e.activation(out=el[:, :, :].rearrange("p n h -> p (n h)"), in_=pD_[:, 0:NB * 4], func=AF.Exp), r=[pD_], w=[el])
                        for n in range(NB):
                            K.op("dve", lambda e: e.tensor_tensor(out=qq[:, n, 0:2, :], in0=ex[:, n, 0:2, :], in1=x_[:, n, 0:256].rearrange("p (o x) -> p o x", o=1).to_broadcast([64, 2, 256]),
                                                                  op=ALU.mult), r=[ex, x_], w=[qq])
                            K.op("dve", lambda e: e.tensor_tensor(out=qq[:, n, 2:4, :], in0=ex[:, n, 2:4, :], in1=kk[:, n, :].rearrange("p (o x) -> p o x", o=1).to_broadcast([64, 2, 256]),
                                                                  op=ALU.mult), r=[ex, kk], w=[qq])
                        for n in range(NB):
                            for a in range(3):
                                for h in range(4):
                                    sl_ = n * 12 + a * 4 + h
                                    K.op("pe", lambda e: e.transpose(ptrf[:, sl_ * 64:(sl_ + 1) * 64], qq[:, n, a, h * 64:(h + 1) * 64], identf3), r=[qq, cst], w=[ptr])
                        K.op("act", lambda e: e.copy(out=trs[:, :, :, :].rearrange("p n s t -> p (n s t)"), in_=ptrf[:, 0:NB * 768]), r=[ptr], w=[trs])
                        for n in range(NB):
                            for h in range(4):
                                c0 = n * 256 + h * 64
                                K.op("pe", lambda e: e.matmul(pc[0:32, 0, c0:c0 + 32], lhsT=trs[:, n, 8 + h, 0:32], rhs=trs[:, n, 4 + h, 0:32], start=True, stop=True), r=[trs], w=[pc])
                                K.op("pe", lambda e: e.matmul(pc[:, 0, c0 + 32:c0 + 64], lhsT=trs[:, n, 8 + h, :], rhs=trs[:, n, 4 + h, 32:64], start=True, stop=True), r=[trs], w=[pc])
                        for n in range(NB):
                            att3 = att[:, n, :].rearrange("p (h t) -> p h t", t=64)
                            pat3 = pc[:, 0, n * 256:(n + 1) * 256].rearrange("p (h t) -> p h t", t=64)
                            K.op("dve", lambda e: e.tensor_tensor(out=att3[0:32, :, 0:32], in0=pat3[0:32, :, 0:32], in1=msk3[0:32, :, 0:32], op=ALU.mult), r=[pc, cst], w=[att])
                            K.op("dve", lambda e: e.tensor_tensor(out=att3[:, :, 32:64], in0=pat3[:, :, 32:64], in1=msk3[:, :, 32:64], op=ALU.mult), r=[pc, cst], w=[att])
                        for n in range(NB):
                            for h in range(4):
                                c0 = n * 256 + h * 64
                                K.op("pe", lambda e: e.matmul(pE[:, c0:c0 + 64], lhsT=att[:, n, h * 64:(h + 1) * 64], rhs=x_[:, n, 512 + h * 64:512 + (h + 1) * 64], start=True, stop=False),
                                     r=[att, x_], w=[pE])
                                K.op("pe", lambda e: e.matmul(pE[:, c0:c0 + 64], lhsT=trs[:, n, h, :], rhs=Sf[:, h, :], start=False, stop=True), r=[trs, Sf], w=[pE])
                            for h in range(4):
                                K.op("pe", lambda e: e.matmul(pc[:, 1, h * 64:(h + 1) * 64], lhsT=qq[:, n, 3, h * 64:(h + 1) * 64], rhs=x_[:, n, 512 + h * 64:512 + (h + 1) * 64], start=True, stop=True),
                                     r=[qq, x_], w=[pc])
                            K.op("dve", lambda e: e.tensor_tensor(out=Sf[:, :, :], in0=Sf[:, :, :], in1=el[:, n, :].rearrange("p (h o) -> p h o", o=1).to_broadcast([64, 4, 64]),
                                                                  op=ALU.mult), r=[Sf, el], w=[Sf])
                            K.op("dve", lambda e: e.tensor_tensor(out=Sf[:, :, :], in0=Sf[:, :, :], in1=pc[:, 1, 0:256].rearrange("p (h d) -> p h d", d=64), op=ALU.add), r=[Sf, pc], w=[Sf])
                        o3 = pE[:, 0:NB * 256].rearrange("p (m d) -> p m d", d=64)
                        tA3 = tA[:, :, :].rearrange("p n (h d) -> p (n h) d", d=64)
                        K.op("act", lambda e: e.activation(out=tA3, in_=o3, func=AF.Square), r=[pE], w=[tA])
                        K.op("dve", lambda e: e.tensor_reduce(out=ssq[:, :, :].rearrange("p n h -> p (n h)"), in_=tA3, axis=AX.X, op=ALU.add), r=[tA], w=[ssq])
                        K.op("act", lambda e: e.activation(out=ssq[:, :, :], in_=ssq[:, :, :], func=AF.Ln, bias=epsb[0:64, :], scale=1.0 / 64), r=[ssq, epsb], w=[ssq])
                        K.op("act", lambda e: e.activation(out=ssq[:, :, :], in_=ssq[:, :, :], func=AF.Exp, scale=-0.5), r=[ssq], w=[ssq])
                        K.op("dve", lambda e: e.tensor_tensor(out=tA3, in0=o3, in1=ssq[:, :, :].rearrange("p n (h o) -> p (n h) o", o=1).to_broadcast([64, NB * 4, 64]), op=ALU.mult),
                             r=[pE, ssq], w=[tA])
                        K.op("dve", lambda e: e.tensor_tensor(out=tA[:, :, :], in0=tA[:, :, :], in1=bcn(gnb[:, :]), op=ALU.mult), r=[tA, gnb], w=[tA])
                        K.op("act", lambda e: e.activation(out=tB[:, :, :], in_=x_[:, :, 768:1024], func=AF.Exp, scale=-1.0), r=[x_], w=[tB])
                        K.op("dve", lambda e: e.tensor_scalar(out=tB[:, :, :], in0=tB[:, :, :], scalar1=1.0, scalar2=None, op0=ALU.add), r=[tB], w=[tB])
                        K.op("dve", lambda e: e.reciprocal(out=tB[:, :, :], in_=tB[:, :, :]), r=[tB], w=[tB])
                        K.op("dve", lambda e: e.tensor_tensor(out=tB[:, :, :], in0=tB[:, :, :], in1=x_[:, :, 768:1024], op=ALU.mult), r=[tB, x_], w=[tB])
                        o_ = ob[gi_ % 2]
                        K.op("dve", lambda e: e.tensor_tensor(out=o_[:, :, :], in0=tA[:, :, :], in1=tB[:, :, :], op=ALU.mult), r=[tA, tB], w=[o_])
                        K.dma("pool", "st", mixed_d[r0:r0 + 64 * NB, 512:768].rearrange("(n p) c -> p n c", p=64), o_[:, :, :], r=[o_])
                    K.phase_end()

            if run("m4a"):
                with ExitStack() as st:
                    cw = SB(st, "m4cw", [128, 24], F32)
                    K.dma("sp", "ld", cw[:, :], cw_d[l, :, 0:24], w=[cw])
                    xin = [SB(st, "m4x%d" % i, [128, 515], F32) for i in range(2)]
                    accs = [SB(st, "m4acc%d" % i, [128, 512], F32) for i in range(2)]
                    sgs = [SB(st, "m4sg%d" % i, [128, 512], F32) for i in range(2)]
                    oo = [SB(st, "m4o%d" % i, [128, 4, 128], F32) for i in range(2)]
                    pt = [PS(st, "m4p%d" % i, [128, 4, 128], F32) for i in range(2)]
                    n = 0
                    for c in range(6):
                        for ti in range(S_ // 512):
                            x_ = xin[n % 2]
                            acc = accs[n % 2]
                            sg = sgs[n % 2]
                            rows = projT_d[640 + c * 128:640 + (c + 1) * 128, :]
                            if ti == 0:
                                K.op("dve", lambda e: e.memset(x_[:, 0:3], 0.0), w=[x_])
                                K.dma("sp", "ld", x_[:, 3:515], rows[:, 0:512], w=[x_])
                            else:
                                K.dma("sp", "ld", x_[:, 0:515], rows[:, ti * 512 - 3:(ti + 1) * 512], w=[x_])
                            K.op("dve", lambda e: e.tensor_scalar(out=acc[:, :], in0=x_[:, 0:512], scalar1=cw[:, c * 4:c * 4 + 1], scalar2=None, op0=ALU.mult), r=[x_, cw], w=[acc])
                            for j in range(1, 4):
                                K.op("dve", lambda e: e.scalar_tensor_tensor(out=acc[:, :], in0=x_[:, j:j + 512], scalar=cw[:, c * 4 + j:c * 4 + j + 1], in1=acc[:, :],
                                                                             op0=ALU.mult, op1=ALU.add), r=[x_, cw, acc], w=[acc])
                            K.op("act", lambda e: e.activation(out=sg[:, :], in_=acc[:, :], func=AF.Exp, scale=-1.0), r=[acc], w=[sg])
                            K.op("dve", lambda e: e.tensor_scalar(out=sg[:, :], in0=sg[:, :], scalar1=1.0, scalar2=None, op0=ALU.add), r=[sg], w=[sg])
                            K.op("dve", lambda e: e.reciprocal(out=sg[:, :], in_=sg[:, :]), r=[sg], w=[sg])
                            K.op("dve", lambda e: e.tensor_tensor(out=acc[:, :], in0=acc[:, :], in1=sg[:, :], op=ALU.mult), r=[acc, sg], w=[acc])
                            p = pt[n % 2]
                            o = oo[n % 2]
                            for s4 in range(4):
                                K.op("pe", lambda e: e.transpose(p[:, s4, :], acc[:, s4 * 128:(s4 + 1) * 128], CF[:, o_id:o_id + 128]), r=[acc, cst], w=[p])
                            K.op("act", lambda e: e.copy(out=o[:, :, :], in_=p[:, :, :]), r=[p], w=[o])
                            K.dma("pool", "st", convK_d[ti * 512:(ti + 1) * 512, c * 128:(c + 1) * 128].rearrange("(s p) c -> p s c", p=128), o[:, :, :], r=[o])
                            n += 1
                K.phase_end()

            if run("m4b"):
                NB = 2
                with ExitStack() as st:
                    nA = SB(st, "g_nA", [64, 4], F32)
                    dtb = SB(st, "g_dtb", [64, 4], F32)
                    gnb = SB(st, "g_gn", [64, 256], F32)
                    K.dma("sp", "ld", nA[:, :], sm[0:1, 712:716].to_broadcast([64, 4]), w=[nA])
                    K.dma("sp", "ld", dtb[:, :], sm[0:1, 716:720].to_broadcast([64, 4]), w=[dtb])
                    for h in range(4):
                        K.dma("sp", "ld", gnb[:, h * 64:(h + 1) * 64], sm[0:1, 720:784].to_broadcast([64, 64]), w=[gnb])
                    K.op("act", lambda e: e.activation(out=nA[:, :], in_=nA[:, :], func=AF.Exp), r=[nA], w=[nA])
                    K.op("dve", lambda e: e.tensor_scalar(out=nA[:, :], in0=nA[:, :], scalar1=-1.0, scalar2=None, op0=ALU.mult), r=[nA], w=[nA])
                    cin = [SB(st, "g_in%d" % i, [64, NB, 768], F32) for i in range(2)]
                    gin = [SB(st, "g_gi%d" % i, [64, NB, 264], F32) for i in range(2)]
                    sq = SB(st, "g_sq", [64, NB, 512], F32)
                    ss = SB(st, "g_ss", [64, NB, 8], F32)
                    qkn = SB(st, "g_qkn", [64, NB, 8, 64], F32)
                    eb = SB(st, "g_eb", [64, NB, 4], F32)
                    lnb = SB(st, "g_lnb", [64, NB, 4], F32)
                    beta = SB(st, "g_beta", [64, NB, 4], F32)
                    g_ = SB(st, "g_g", [64, NB, 4], F32)
                    gcols = SB(st, "g_gc", [64, 2, NB, 4, 64], F32)
                    G = SB(st, "g_G", [64, NB, 4], F32)
                    eG = SB(st, "g_eG", [64, 3, NB, 4], F32)
                    DT = SB(st, "g_DT", [64, NB, 4, 2, 64], F32)
                    LF = SB(st, "g_LF", [64, NB, 4, 2, 64], F32)
                    qdec = SB(st, "g_qdec", [64, NB, 4, 64], F32)
                    tA4 = SB(st, "g_tA4", [64, NB, 4, 64], F32)
                    kbg = SB(st, "g_kbg", [64, NB, 4, 64], F32)
                    kdec = SB(st, "g_kdec", [64, NB, 4, 64], F32)
                    vbt = SB(st, "g_vb", [64, NB, 4, 64], F32)
                    trs = SB(st, "g_tr", [64, NB, 3, 4, 64], F32)
                    qkT = SB(st, "g_qkT", [64, NB, 4, 2, 64], F32)
                    Mx = SB(st, "g_M", [64, NB, 4, 2, 64], F32)
                    QKm = SB(st, "g_QKm", [64, NB, 4, 64], F32)
                    XX = [SB(st, "g_XX%d" % i, [64, 2, NB, 4, 64], F32) for i in range(2)]
                    Tm = SB(st, "g_Tm", [64, NB, 4, 64], F32)
                    nWT = SB(st, "g_nWT", [64, NB, 4, 64], F32)
                    vnew = SB(st, "g_vn", [64, 4, 64], F32)
                    Sf = SB(st, "g_S", [64, 4, 64], F32)
                    tA = SB(st, "g_tA", [64, NB, 256], F32)
                    tB = SB(st, "g_tB", [64, NB, 256], F32)
                    ssq = SB(st, "g_ssq", [64, NB, 4], F32)
                    ob = [SB(st, "g_ob%d" % i, [64, NB, 256], BF16) for i in range(2)]
                    bD = PS(st, "g_bD", [64, NB, 512], F32)
                    bT = PS(st, "g_bT", [64, 3, 512], F32)
                    bX = PS(st, "g_bX", [64, 2, 512], F32)
                    bU = PS(st, "g_bU", [64, 512], F32)
                    K.op("dve", lambda e: e.memset(Sf[:, :, :], 0.0), w=[Sf])
                    identf = CF[0:64, o_id:o_id + 64]
                    bD5 = bD[:, :, :].rearrange("p n (h a t) -> p n h a t", h=4, a=2)
                    bTf = bT[:, :, :].rearrange("p b x -> p (b x)")
                    ng = S_ // (64 * NB)
                    for gi_ in range(ng):
                        r0 = gi_ * 64 * NB
                        x_ = cin[gi_ % 2]
                        gi = gin[gi_ % 2]
                        K.dma("sp", "ld", x_[:, :, :], convK_d[r0:r0 + 64 * NB, :].rearrange("(n p) c -> p n c", p=64), w=[x_])
                        K.dma("sp", "ld", gi[:, :, :], projK_d[r0:r0 + 64 * NB, 1152:1416].rearrange("(n p) c -> p n c", p=64), w=[gi])
                        xqk = x_[:, :, 0:512].rearrange("p n (h d) -> p n h d", d=64)
                        xv = x_[:, :, 512:768].rearrange("p n (h d) -> p n h d", d=64)
                        K.op("act", lambda e: e.activation(out=sq[:, :, :], in_=x_[:, :, 0:512], func=AF.Square), r=[x_], w=[sq])
                        K.op("dve", lambda e: e.tensor_reduce(out=ss[:, :, :].rearrange("p n h -> p (n h)"), in_=sq[:, :, :].rearrange("p n (h d) -> p (n h) d", d=64),
                                                              axis=AX.X, op=ALU.add), r=[sq], w=[ss])
                        K.op("act", lambda e: e.activation(out=ss[:, :, :], in_=ss[:, :, :], func=AF.Ln, bias=epsb[0:64, :], scale=1.0), r=[ss, epsb], w=[ss])
                        K.op("act", lambda e: e.activation(out=ss[:, :, :], in_=ss[:, :, :], func=AF.Exp, scale=-0.5), r=[ss], w=[ss])
                        K.op("dve", lambda e: e.tensor_scalar(out=ss[:, :, 0:4], in0=ss[:, :, 0:4], scalar1=0.125, scalar2=None, op0=ALU.mult), r=[ss], w=[ss])
                        for n in range(NB):
                            K.op("dve", lambda e: e.tensor_tensor(out=qkn[:, n, :, :], in0=xqk[:, n, :, :],
                                                                  in1=ss[:, n, :].rearrange("p (h o) -> p h o", o=1).to_broadcast([64, 8, 64]), op=ALU.mult), r=[x_, ss], w=[qkn])
                        K.op("act", lambda e: e.activation(out=eb[:, :, :], in_=gi[:, :, 256:260], func=AF.Exp, scale=-1.0), r=[gi], w=[eb])
                        K.op("dve", lambda e: e.tensor_scalar(out=eb[:, :, :], in0=eb[:, :, :], scalar1=1.0, scalar2=None, op0=ALU.add), r=[eb], w=[eb])
                        K.op("dve", lambda e: e.reciprocal(out=beta[:, :, :], in_=eb[:, :, :]), r=[eb], w=[beta])
                        K.op("act", lambda e: e.activation(out=lnb[:, :, :], in_=beta[:, :, :], func=AF.Ln), r=[beta], w=[lnb])
                        K.op("dve", lambda e: e.tensor_tensor(out=g_[:, :, :], in0=gi[:, :, 260:264], in1=dtb[:, :].rearrange("p (o h) -> p o h", o=1).to_broadcast([64, NB, 4]),
                                                              op=ALU.add), r=[gi, dtb], w=[g_])
                        K.op("act", lambda e: e.activation(out=g_[:, :, :], in_=g_[:, :, :], func=AF.Exp), r=[g_], w=[g_])
                        K.op("dve", lambda e: e.tensor_scalar(out=g_[:, :, :], in0=g_[:, :, :], scalar1=1.0, scalar2=None, op0=ALU.add), r=[g_], w=[g_])
                        K.op("act", lambda e: e.activation(out=g_[:, :, :], in_=g_[:, :, :], func=AF.Ln), r=[g_], w=[g_])
                        K.op("dve", lambda e: e.tensor_tensor(out=g_[:, :, :], in0=g_[:, :, :], in1=nA[:, :].rearrange("p (o h) -> p o h", o=1).to_broadcast([64, NB, 4]),
                                                              op=ALU.mult), r=[g_, nA], w=[g_])
                        gfl = g_[:, :, :].rearrange("p n h -> p (n h)")
                        W4 = NB * 4
                        K.op("pe", lambda e: e.matmul(bU[:, 0:W4], lhsT=CF[0:64, o_cum:o_cum + 64], rhs=gfl, start=True, stop=True), r=[cst, g_], w=[bU])
                        K.op("pe", lambda e: e.matmul(bU[:, W4:2 * W4], lhsT=CF[0:64, o_lc:o_lc + 64], rhs=gfl, start=True, stop=True), r=[cst, g_], w=[bU])
                        K.op("pe", lambda e: e.matmul(bU[:, 2 * W4:3 * W4], lhsT=CF[0:64, o_on:o_on + 64], rhs=gfl, start=True, stop=True), r=[cst, g_], w=[bU])
                        K.op("dve", lambda e: e.tensor_copy(out=G[:, :, :].rearrange("p n h -> p (n h)"), in_=bU[:, 0:W4]), r=[bU], w=[G])
                        K.op("act", lambda e: e.activation(out=eG[:, :, :, :].rearrange("p a n h -> p (a n h)"), in_=bU[:, 0:3 * W4], func=AF.Exp), r=[bU], w=[eG])
                        for n in range(NB):
                            K.op("dve", lambda e: e.tensor_copy(out=gcols[:, 0, n, :, :], in_=g_[:, n, :].rearrange("p (h o) -> p h o", o=1).to_broadcast([64, 4, 64])), r=[g_], w=[gcols])
                            K.op("dve", lambda e: e.tensor_copy(out=gcols[:, 1, n, :, :], in_=lnb[:, n, :].rearrange("p (h o) -> p h o", o=1).to_broadcast([64, 4, 64])), r=[lnb], w=[gcols])
                        for n in range(NB):
                            for h in range(4):
                                K.op("pe", lambda e: e.matmul(bD5[:, n, h, 0, :], lhsT=gcols[:, 0, n, h, :], rhs=CF[0:64, o_cum:o_cum + 64], start=True, stop=True), r=[gcols, cst], w=[bD])
                                K.op("pe", lambda e: e.matmul(bD5[:, n, h, 1, :], lhsT=gcols[:, 0, n, h, :], rhs=CF[0:64, o_cum:o_cum + 64], start=True, stop=False), r=[gcols, cst], w=[bD])
                                K.op("pe", lambda e: e.matmul(bD5[:, n, h, 1, :], lhsT=gcols[:, 1, n, h, :], rhs=identf, start=False, stop=True), r=[gcols, cst], w=[bD])
                        for n in range(NB):
                            for sl in range(2):
                                K.op("dve", lambda e: e.tensor_tensor(out=DT[:, n, :, sl, :], in0=bD5[:, n, :, sl, :],
                                                                      in1=G[:, n, :].rearrange("p (h o) -> p h o", o=1).to_broadcast([64, 4, 64]), op=ALU.subtract), r=[bD, G], w=[DT])
                        K.op("dve", lambda e: e.tensor_tensor(out=DT[:, :, :, :, :].rearrange("p n h a t -> p n (h a t)"), in0=DT[:, :, :, :, :].rearrange("p n h a t -> p n (h a t)"),
                                                              in1=CF[0:64, o_cap:o_cap + 512].rearrange("p (o x) -> p o x", o=1).to_broadcast([64, NB, 512]), op=ALU.min), r=[DT, cst], w=[DT])
                        K.op("act", lambda e: e.activation(out=LF[:, :, :, :, :].rearrange("p n h a t -> p (n h a t)"), in_=DT[:, :, :, :, :].rearrange("p n h a t -> p (n h a t)"),
                                                           func=AF.Exp), r=[DT], w=[LF])
                        for n in range(NB):
                            bc = lambda t2: t2.rearrange("p (h o) -> p h o", o=1).to_broadcast([64, 4, 64])
                            K.op("dve", lambda e: e.tensor_tensor(out=qdec[:, n, :, :], in0=qkn[:, n, 0:4, :], in1=bc(eG[:, 0, n, :]), op=ALU.mult), r=[qkn, eG], w=[qdec])
                            K.op("dve", lambda e: e.tensor_tensor(out=tA4[:, n, :, :], in0=qkn[:, n, 4:8, :], in1=bc(beta[:, n, :]), op=ALU.mult), r=[qkn, beta], w=[tA4])
                            K.op("dve", lambda e: e.tensor_tensor(out=kbg[:, n, :, :], in0=tA4[:, n, :, :], in1=bc(eG[:, 0, n, :]), op=ALU.mult), r=[tA4, eG], w=[kbg])
                            K.op("pool", lambda e: e.tensor_tensor(out=kdec[:, n, :, :], in0=qkn[:, n, 4:8, :], in1=bc(eG[:, 1, n, :]), op=ALU.mult), r=[qkn, eG], w=[kdec])
                            K.op("pool", lambda e: e.tensor_tensor(out=vbt[:, n, :, :], in0=xv[:, n, :, :], in1=bc(beta[:, n, :]), op=ALU.mult), r=[x_, beta], w=[vbt])
                        for n in range(NB):
                            for a in range(3):
                                for h in range(4):
                                    src = qkn[:, n, h, :] if a == 0 else (qkn[:, n, 4 + h, :] if a == 1 else qdec[:, n, h, :])
                                    s_ = n * 12 + a * 4 + h
                                    K.op("pe", lambda e: e.transpose(bTf[:, s_ * 64:(s_ + 1) * 64], src, identf), r=[qkn, qdec, cst], w=[bT])
                        K.op("act", lambda e: e.copy(out=trs[:, :, :, :, :].rearrange("p n a h t -> p (n a h t)"), in_=bTf[:, 0:NB * 768]), r=[bT], w=[trs])
                        for n in range(NB):
                            K.op("pool", lambda e: e.tensor_copy(out=qkT[:, n, :, :, :], in_=trs[:, n, 0:2, :, :].rearrange("p a h t -> p h a t")), r=[trs], w=[qkT])
                        for n in range(NB):
                            for h in range(4):
                                K.op("pe", lambda e: e.matmul(bD5[:, n, h, :, :], lhsT=trs[:, n, 1, h, :], rhs=qkT[:, n, h, :, :], start=True, stop=True), r=[trs, qkT], w=[bD])
                        K.op("dve", lambda e: e.tensor_tensor(out=Mx[:, :, :, :, :].rearrange("p n h a t -> p n (h a t)"), in0=bD[:, :, :],
                                                              in1=LF[:, :, :, :, :].rearrange("p n h a t -> p n (h a t)"), op=ALU.mult), r=[bD, LF], w=[Mx])
                        X0 = XX[0]
                        for n in range(NB):
                            K.op("act", lambda e: e.copy(out=QKm[:, n, :, :], in_=Mx[:, n, :, 0, :]), r=[Mx], w=[QKm])
                            K.op("dve", lambda e: e.tensor_scalar(out=X0[:, 0, n, :, :], in0=Mx[:, n, :, 1, :], scalar1=-1.0, scalar2=None, op0=ALU.mult), r=[Mx], w=[X0])
                            K.op("pool", lambda e: e.scalar_tensor_tensor(out=Tm[:, n, :, :], in0=Mx[:, n, :, 1, :], scalar=-1.0,
                                                                         in1=identf.rearrange("p (o t) -> p o t", o=1).to_broadcast([64, 4, 64]), op0=ALU.mult, op1=ALU.add),
                                 r=[Mx, cst], w=[Tm]) if False else \
                                K.op("dve", lambda e: e.scalar_tensor_tensor(out=Tm[:, n, :, :], in0=Mx[:, n, :, 1, :], scalar=-1.0,
                                                                             in1=identf.rearrange("p (o t) -> p o t", o=1).to_broadcast([64, 4, 64]), op0=ALU.mult, op1=ALU.add),
                                     r=[Mx, cst], w=[Tm])
                        for n in range(NB):
                            for h in range(4):
                                K.op("pe", lambda e: e.transpose(bTf[:, (n * 4 + h) * 64:(n * 4 + h + 1) * 64], X0[:, 0, n, h, :], identf), r=[X0, cst], w=[bT])
                        K.op("act", lambda e: e.copy(out=X0[:, 1, :, :, :].rearrange("p n h t -> p (n h t)"), in_=bTf[:, 0:NB * 256]), r=[bT], w=[X0])
                        for step in range(5):
                            last = step == 4
                            cur = XX[step % 2]
                            nxt = XX[(step + 1) % 2]
                            for n in range(NB):
                                for h in range(4):
                                    c0 = (n * 4 + h) * 64
                                    K.op("pe", lambda e: e.matmul(bX[:, 1, c0:c0 + 64], lhsT=cur[:, 0, n, h, :], rhs=cur[:, 1, n, h, :], start=True, stop=True), r=[cur], w=[bX])
                                    if not last:
                                        K.op("pe", lambda e: e.matmul(bX[:, 0, c0:c0 + 64], lhsT=cur[:, 1, n, h, :], rhs=cur[:, 0, n, h, :], start=True, stop=True), r=[cur], w=[bX])
                            if last:
                                K.op("act", lambda e: e.copy(out=nxt[:, 1, :, :, :].rearrange("p n h t -> p (n h t)"), in_=bX[:, 1, 0:NB * 256]), r=[bX], w=[nxt])
                            else:
                                K.op("act", lambda e: e.copy(out=nxt[:, :, :, :, :].rearrange("p a n h t -> p a (n h t)"), in_=bX[:, :, 0:NB * 256]), r=[bX], w=[nxt])
                            for n in range(NB):
                                for h in range(4):
                                    c0 = (n * 4 + h) * 64
                                    K.op("pe", lambda e: e.matmul(bU[:, c0:c0 + 64], lhsT=nxt[:, 1, n, h, :], rhs=Tm[:, n, h, :], start=True, stop=True), r=[nxt, Tm], w=[bU])
                            K.op("dve", lambda e: e.tensor_tensor(out=Tm[:, :, :, :].rearrange("p n h t -> p (n h t)"), in0=Tm[:, :, :, :].rearrange("p n h t -> p (n h t)"),
                                                                  in1=bU[:, 0:NB * 256], op=ALU.add), r=[Tm, bU], w=[Tm])
                        for n in range(NB):
                            for h in range(4):
                                c0 = (n * 4 + h) * 64
                                K.op("pe", lambda e: e.matmul(bU[:, c0:c0 + 64], lhsT=kbg[:, n, h, :], rhs=Tm[:, n, h, :], start=True, stop=True), r=[kbg, Tm], w=[bU])
                        K.op("dve", lambda e: e.tensor_scalar(out=nWT[:, :, :, :].rearrange("p n h t -> p (n h t)"), in0=bU[:, 0:NB * 256], scalar1=-1.0, scalar2=None, op0=ALU.mult),
                             r=[bU], w=[nWT])
                        for n in range(NB):
                            for h in range(4):
                                K.op("pe", lambda e: e.matmul(bT[:, 1, h * 64:(h + 1) * 64], lhsT=Tm[:, n, h, :], rhs=vbt[:, n, h, :], start=True, stop=False), r=[Tm, vbt], w=[bT])
                                K.op("pe", lambda e: e.matmul(bT[:, 1, h * 64:(h + 1) * 64], lhsT=nWT[:, n, h, :], rhs=Sf[:, h, :], start=False, stop=True), r=[nWT, Sf], w=[bT])
                            K.op("act", lambda e: e.copy(out=vnew[:, :, :].rearrange("p h d -> p (h d)"), in_=bT[:, 1, 0:256]), r=[bT], w=[vnew])
                            for h in range(4):
                                c0 = (n * 4 + h) * 64
                                K.op("pe", lambda e: e.matmul(bT[:, 2, c0:c0 + 64], lhsT=trs[:, n, 2, h, :], rhs=Sf[:, h, :], start=True, stop=False), r=[trs, Sf], w=[bT])
                                K.op("pe", lambda e: e.matmul(bT[:, 2, c0:c0 + 64], lhsT=QKm[:, n, h, :], rhs=vnew[:, h, :], start=False, stop=True), r=[QKm, vnew], w=[bT])
                            for h in range(4):
                                K.op("pe", lambda e: e.matmul(bT[:, 1, 256 + h * 64:256 + (h + 1) * 64], lhsT=kdec[:, n, h, :], rhs=vnew[:, h, :], start=True, stop=True), r=[kdec, vnew], w=[bT])
                            K.op("dve", lambda e: e.tensor_tensor(out=Sf[:, :, :], in0=Sf[:, :, :], in1=eG[:, 2, n, :].rearrange("p (h o) -> p h o", o=1).to_broadcast([64, 4, 64]),
                                                                  op=ALU.mult), r=[Sf, eG], w=[Sf])
                            K.op("dve", lambda e: e.tensor_tensor(out=Sf[:, :, :], in0=Sf[:, :, :], in1=bT[:, 1, 256:512].rearrange("p (h d) -> p h d", d=64), op=ALU.add),
                                 r=[Sf, bT], w=[Sf])
                        o3 = bT[:, 2, 0:NB * 256].rearrange("p (m d) -> p m d", d=64)
                        tA3 = tA[:, :, :].rearrange("p n (h d) -> p (n h) d", d=64)
                        K.op("act", lambda e: e.activation(out=tA3, in_=o3, func=AF.Square), r=[bT], w=[tA])
                        K.op("dve", lambda e: e.tensor_reduce(out=ssq[:, :, :].rearrange("p n h -> p (n h)"), in_=tA3, axis=AX.X, op=ALU.add), r=[tA], w=[ssq])
                        K.op("act", lambda e: e.activation(out=ssq[:, :, :], in_=ssq[:, :, :], func=AF.Ln, bias=epsb[0:64, :], scale=1.0 / 64), r=[ssq, epsb], w=[ssq])
                        K.op("act", lambda e: e.activation(out=ssq[:, :, :], in_=ssq[:, :, :], func=AF.Exp, scale=-0.5), r=[ssq], w=[ssq])
                        K.op("dve", lambda e: e.tensor_tensor(out=tA3, in0=o3, in1=ssq[:, :, :].rearrange("p n (h o) -> p (n h) o", o=1).to_broadcast([64, NB * 4, 64]), op=ALU.mult),
                             r=[bT, ssq], w=[tA])
                        K.op("dve", lambda e: e.tensor_tensor(out=tA[:, :, :], in0=tA[:, :, :], in1=gnb[:, :].rearrange("p (o x) -> p o x", o=1).to_broadcast([64, NB, 256]), op=ALU.mult),
                             r=[tA, gnb], w=[tA])
                        K.op("act", lambda e: e.activation(out=tB[:, :, :], in_=gi[:, :, 0:256], func=AF.Exp, scale=-1.0), r=[gi], w=[tB])
                        K.op("dve", lambda e: e.tensor_scalar(out=tB[:, :, :], in0=tB[:, :, :], scalar1=1.0, scalar2=None, op0=ALU.add), r=[tB], w=[tB])
                        K.op("dve", lambda e: e.reciprocal(out=tB[:, :, :], in_=tB[:, :, :]), r=[tB], w=[tB])
                        K.op("dve", lambda e: e.tensor_tensor(out=tB[:, :, :], in0=tB[:, :, :], in1=gi[:, :, 0:256], op=ALU.mult), r=[tB, gi], w=[tB])
                        o_ = ob[gi_ % 2]
                        K.op("dve", lambda e: e.tensor_tensor(out=o_[:, :, :], in0=tA[:, :, :], in1=tB[:, :, :], op=ALU.mult), r=[tA, tB], w=[o_])
                        K.dma("pool", "st", mixed_d[r0:r0 + 64 * NB, 768:1024].rearrange("(n p) c -> p n c", p=64), o_[:, :, :], r=[o_])
                    K.phase_end()

            if run("m5"):
                with ExitStack() as st:
                    wo = SB(st, "m5w", [128, 8, D_], BF16)
                    load_weight_bf16(st, "m5w", lambda kc: wout_d[l, kc * 128:(kc + 1) * 128, :], 8, D_, wo, 512)
                    mk = [SB(st, "m5m%d" % i, [128, D_], BF16) for i in range(2)]
                    mT = SB(st, "m5mT", [128, 8, 512], BF16)
                    xt = [SB(st, "m5x%d" % i, [128, 8, 512], F32) for i in range(2)]
                    ptr = [PS(st, "m5pt%d" % i, [128, 8, 128], BF16) for i in range(2)]
                    py = [PS(st, "m5py%d" % i, [128, 512], F32) for i in range(2)]
                    n = 0
                    for ti in range(S_ // 512):
                        x_ = xt[ti % 2]
                        K.dma("sp", "ld", x_[:, :, :], xT_v[:, :, ti * 512:(ti + 1) * 512], w=[x_])
                        for sub in range(4):
                            m_ = mk[n % 2]
                            p = ptr[n % 2]
                            r0 = ti * 512 + sub * 128
                            K.dma("sp", "ld", m_[:, :], mixed_d[r0:r0 + 128, :], w=[m_])
                            for c in range(8):
                                K.op("pe", lambda e: e.transpose(p[:, c, :], m_[:, c * 128:(c + 1) * 128], CB[:, o_id:o_id + 128]), r=[m_, cstb], w=[p])
                            if n % 2:
                                K.op("act", lambda e: e.copy(out=mT[:, :, sub * 128:(sub + 1) * 128], in_=p[:, :, :]), r=[p], w=[mT])
                            else:
                                K.op("dve", lambda e: e.tensor_copy(out=mT[:, :, sub * 128:(sub + 1) * 128], in_=p[:, :, :]), r=[p], w=[mT])
                            n += 1
                        for dc in range(8):
                            p = py[dc % 2]
                            for kc in range(8):
                                K.op("pe", lambda e: e.matmul(p[:, :], lhsT=wo[:, kc, dc * 128:(dc + 1) * 128], rhs=mT[:, kc, :], start=(kc == 0), stop=(kc == 7)),
                                     r=[wo, mT], w=[p], inc=(kc == 7))
                            K.op("dve", lambda e: e.scalar_tensor_tensor(out=x_[:, dc, :], in0=p[:, :], scalar=modt[:, 16 + dc:17 + dc], in1=x_[:, dc, :], op0=ALU.mult, op1=ALU.add),
                                 r=[p, modt, x_], w=[x_])
                        K.dma("pool", "st", xT_v[:, :, ti * 512:(ti + 1) * 512], x_[:, :, :], r=[x_])
                K.phase_end()

            if run("ffn"):
                with ExitStack() as st:
                    wg = SB(st, "fwg", [128, 8, DFF], BF16)
                    wu = SB(st, "fwu", [128, 8, DFF], BF16)
                    wd = SB(st, "fwd", [128, 22, D_], BF16)
                    with ExitStack() as st2:
                        load_weight_bf16(st2, "fwg", lambda kc: wg_d[l, kc * 128:(kc + 1) * 128, :], 8, DFF, wg, 704)
                        load_weight_bf16(st2, "fwu", lambda kc: wu_d[l, kc * 128:(kc + 1) * 128, :], 8, DFF, wu, 704)
                        load_weight_bf16(st2, "fwd", lambda kc: wd_d[l, kc * 128:(kc + 1) * 128, :], 22, D_, wd, 512)
                        K.phase_end()
                    TF = 256
                    xt = [SB(st, "fx%d" % i, [128, 8, TF], F32) for i in range(2)]
                    hT = [SB(st, "fh%d" % i, [128, 8, TF], BF16) for i in range(2)]
                    sq = [SB(st, "fsq%d" % i, [128, 8, TF], BF16) for i in range(2)]
                    rstd = [SB(st, "fr%d" % i, [128, TF], F32) for i in range(2)]
                    tmpf = [SB(st, "ft%d" % i, [128, TF], F32) for i in range(2)]
                    aT = SB(st, "fa", [128, 22, TF], BF16)
                    sg = [SB(st, "fsg%d" % i, [128, TF], F32) for i in range(2)]
                    pss = PS(st, "fpss", [128, 512], F32)
                    pg = [PS(st, "fpg%d" % i, [128, 512], F32) for i in range(2)]
                    pu = [PS(st, "fpu%d" % i, [128, 512], F32) for i in range(2)]
                    py = [PS(st, "fpy%d" % i, [128, 512], F32) for i in range(2)]
                    NT = S_ // TF

                    def f_load(ti):
                        x_ = xt[ti % 2]
                        K.dma("sp", "ld", x_[:, :, :], xT_v[:, :, ti * TF:(ti + 1) * TF], w=[x_])

                    def f_rms(ti):
                        x_ = xt[ti % 2]
                        h_ = hT[ti % 2]
                        s_ = sq[ti % 2]
                        r_ = rstd[ti % 2]
                        K.op("act", lambda e: e.activation(out=s_[:, :, :], in_=x_[:, :, :], func=AF.Square), r=[x_], w=[s_])
                        for c in range(8):
                            K.op("pe", lambda e: e.matmul(pss[:, 0:TF], lhsT=onesb[:, :], rhs=s_[:, c, :], start=(c == 0), stop=(c == 7)), r=[onesb, s_], w=[pss], inc=(c == 7))
                        K.op("act", lambda e: e.activation(out=r_[:, :], in_=pss[:, 0:TF], func=AF.Sqrt, bias=epsb[:, :], scale=1.0 / D_), r=[pss, epsb], w=[r_])
                        K.op("dve", lambda e: e.reciprocal(out=r_[:, :], in_=r_[:, :]), r=[r_], w=[r_])
                        for c in range(8):
                            t = tmpf[c % 2]
                            K.op("dve", lambda e: e.tensor_tensor(out=t[:, :], in0=x_[:, c, :], in1=r_[:, :], op=ALU.mult), r=[x_, r_], w=[t])
                            K.op("act", lambda e: e.activation(out=h_[:, c, :], in_=t[:, :], func=AF.Identity, bias=modt[:, 24 + c:25 + c], scale=gscf[:, c:c + 1]),
                                 r=[t, modt, gscf], w=[h_])

                    f_load(0)
                    f_rms(0)
                    for ti in range(NT):
                        x_ = xt[ti % 2]
                        h_ = hT[ti % 2]
                        if ti + 1 < NT:
                            f_load(ti + 1)
                        for fc in range(22):
                            g_ = pg[fc % 2]
                            u_ = pu[fc % 2]
                            s_ = sg[fc % 2]
                            for kc in range(8):
                                K.op("pe", lambda e: e.matmul(g_[:, 0:TF], lhsT=wg[:, kc, fc * 128:(fc + 1) * 128], rhs=h_[:, kc, :], start=(kc == 0), stop=(kc == 7)), r=[wg, h_], w=[g_], inc=(kc == 7))
                            for kc in range(8):
                                K.op("pe", lambda e: e.matmul(u_[:, 0:TF], lhsT=wu[:, kc, fc * 128:(fc + 1) * 128], rhs=h_[:, kc, :], start=(kc == 0), stop=(kc == 7)), r=[wu, h_], w=[u_], inc=(kc == 7))
                            K.op("act", lambda e: e.activation(out=s_[:, :], in_=g_[:, 0:TF], func=AF.Silu), r=[g_], w=[s_])
                            K.op("dve", lambda e: e.tensor_tensor(out=aT[:, fc, :], in0=s_[:, :], in1=u_[:, 0:TF], op=ALU.mult), r=[s_, u_], w=[aT])
                        if ti + 1 < NT:
                            f_rms(ti + 1)
                        for dc in range(8):
                            p = py[dc % 2]
                            for fc in range(22):
                                K.op("pe", lambda e: e.matmul(p[:, 0:TF], lhsT=wd[:, fc, dc * 128:(dc + 1) * 128], rhs=aT[:, fc, :], start=(fc == 0), stop=(fc == 21)), r=[wd, aT], w=[p], inc=(fc == 21))
                            K.op("dve", lambda e: e.scalar_tensor_tensor(out=x_[:, dc, :], in0=p[:, 0:TF], scalar=modt[:, 40 + dc:41 + dc], in1=x_[:, dc, :], op0=ALU.mult, op1=ALU.add),
                                 r=[p, modt, x_], w=[x_])
                        K.dma("pool", "st", xT_v[:, :, ti * TF:(ti + 1) * TF], x_[:, :, :], r=[x_])
                K.phase_end()

        if run("pout"):
            with ExitStack() as st:
                xin = [SB(st, "pox%d" % i, [128, 8, 128], F32) for i in range(2)]
                xo = [SB(st, "poo%d" % i, [128, D_], F32) for i in range(2)]
                pt = [PS(st, "pop%d" % i, [128, D_], F32) for i in range(2)]
                last = None
                for i in range(S_ // 128):
                    a = xin[i % 2]
                    o = xo[i % 2]
                    p = pt[i % 2]
                    K.dma("sp", "ld", a[:, :, :], xT_d.rearrange("(c p) t -> p c t", p=128)[:, :, i * 128:(i + 1) * 128], w=[a])
                    for c in range(8):
                        K.op("pe", lambda e: e.transpose(p[:, c * 128:(c + 1) * 128], a[:, c, :], CF[:, COFF["ident"][0]:COFF["ident"][0] + 128]), r=[a, cst], w=[p])
                    if i % 2:
                        K.op("act", lambda e: e.copy(out=o[:, :], in_=p[:, :]), r=[p], w=[o])
                    else:
                        K.op("dve", lambda e: e.tensor_copy(out=o[:, :], in_=p[:, :]), r=[p], w=[o])
                    last = K.dma("pool", "st", out_d[i * 128:(i + 1) * 128, :], o[:, :], r=[o])
        K.phase_end()
        print("instructions emitted:", K.ninstr, "sems:", K.nsem)
    return nc


def make_in_maps(inputs):
    f = lambda a: np.ascontiguousarray(np.asarray(a, dtype=np.float32))
    x = f(inputs["x"])
    c = f(inputs["c"])
    pos = np.ascontiguousarray(np.asarray(inputs["positions"], dtype=np.int32))
    fm = lambda v: np.ascontiguousarray(v.reshape(-1, 128).T)
    ada_b = f(inputs["ada_b"])
    adabT = np.stack([fm(ada_b[l]) for l in range(NL)])
    nmixT = np.stack([fm(f(inputs["norm_mix"])[l]) for l in range(NL)])
    nffnT = np.stack([fm(f(inputs["norm_ffn"])[l]) for l in range(NL)])
    small = np.zeros((NL, 1, 1024), np.float32)
    for l in range(NL):
        small[l, 0, 0:64] = f(inputs["attn_q_norm"])[l]
        small[l, 0, 64:128] = f(inputs["attn_k_norm"])[l]
        small[l, 0, 128:136] = f(inputs["attn_sinks"])[l]
        small[l, 0, 136:392] = f(inputs["hgrn_lb_logits"])[l]
        small[l, 0, 648:712] = f(inputs["hgrn_out_norm"])[l]
        small[l, 0, 712:716] = f(inputs["gdn_a_log"])[l]
        small[l, 0, 716:720] = f(inputs["gdn_dt_bias"])[l]
        small[l, 0, 720:784] = f(inputs["gdn_out_norm"])[l]
    cw = f(inputs["gdn_conv_w"])
    convT = np.zeros((NL, 128, 26), np.float32)
    convT[:, :, 0:24] = cw.reshape(NL, 4, 6, 128).transpose(0, 3, 2, 1).reshape(NL, 128, 24)
    convT[:, 0:64, 24] = f(inputs["attn_q_norm"])
    convT[:, 0:64, 25] = f(inputs["attn_k_norm"])
    shared = {
        "ada_w": f(inputs["ada_w"]), "ada_bT": adabT, "nmixT": nmixT, "nffnT": nffnT,
        "w_in": f(inputs["w_in"]), "w_out": f(inputs["w_out"]), "w_gate": f(inputs["w_gate"]),
        "w_up": f(inputs["w_up"]), "w_down": f(inputs["w_down"]), "small": small, "convT": convT, "cst": CST,
    }
    maps = []
    for b in range(8):
        m = dict(shared)
        m["x"] = np.ascontiguousarray(x[b])
        m["cT"] = fm(c[b])
        m["pos"] = np.ascontiguousarray(pos[b:b + 1])
        maps.append(m)
    return maps


def kernel(**inputs):
    nc = build()
    maps = make_in_maps(inputs)
    res = run_bass_kernel_spmd(nc, maps, core_ids=list(range(8)))
    return np.stack([np.asarray(r["out"], dtype=np.float32) for r in res.results], axis=0)
```

```python
import os
import numpy as np
from contextlib import ExitStack
import concourse.bass as bass
import concourse.mybir as mybir
from concourse.bass_utils import run_bass_kernel_spmd

F32 = mybir.dt.float32
BF16 = mybir.dt.bfloat16
I32 = mybir.dt.int32
AF = mybir.ActivationFunctionType
ALU = mybir.AluOpType
AX = mybir.AxisListType

S_ = 4096
D_ = 1024
DFF = 2816
DIN = 2824
NL = 2
EPS = 1e-6
NEG = -30000.0


class Buf:
    __slots__ = ("t", "w", "r")

    def __init__(self, t):
        self.t = t
        self.w = {}
        self.r = {}

    def __getitem__(self, k):
        return self.t[k]


class KB:
    EPOCH = 16000
    DMAX = 16 * 1500

    def __init__(self, nc, stack):
        self.nc = nc
        self.stack = stack
        self.eng = {"pe": nc.tensor, "act": nc.scalar, "dve": nc.vector, "pool": nc.gpsimd, "sp": nc.sync}
        self.sem = {}
        self.cnt = {}
        self.epoch = {}
        self.waited = {}
        self.nsem = 0
        for k in self.eng:
            self.epoch[k] = 0
            self.cnt[k] = 0
            self.sem[(k, 0)] = self._newsem(k + "_0")
        self.dsem = {}
        self.dcnt = {}
        self.dcur = {}
        self.dfree = []
        self.ninstr = 0

    def _newsem(self, name):
        self.nsem += 1
        return self.stack.enter_context(self.nc.semaphore("s_" + name))

    def _wait(self, k, dep):
        if dep is None:
            return
        dk, de, dn = dep
        if dk == k and k == "pe":
            return
        key = (k, dk, de)
        if self.waited.get(key, 0) >= dn:
            return
        self.waited[key] = dn
        s = self.dsem[dk] if dk in self.dsem else self.sem[(dk, de)]
        self.eng[k].wait_ge(s, dn)
        self.ninstr += 1

    def _deps(self, k, r, w):
        for b in r:
            for tok in b.w.values():
                self._wait(k, tok)
        for b in w:
            for tok in b.w.values():
                self._wait(k, tok)
            for tok in b.r.values():
                self._wait(k, tok)

    def _mark(self, tok, r, w):
        for b in r:
            b.r[(tok[0], tok[1])] = tok
        for b in w:
            b.w[(tok[0], tok[1])] = tok
            b.r = {}

    def op(self, k, fn, r=(), w=(), inc=True):
        self._deps(k, r, w)
        if inc and self.cnt[k] >= self.EPOCH:
            self.epoch[k] += 1
            self.cnt[k] = 0
            self.sem[(k, self.epoch[k])] = self._newsem("%s_%d" % (k, self.epoch[k]))
        ins = fn(self.eng[k])
        self.ninstr += 1
        if inc:
            self.cnt[k] += 1
            ins.then_inc(self.sem[(k, self.epoch[k])], 1)
            tok = (k, self.epoch[k], self.cnt[k])
        else:
            tok = (k, self.epoch[k], self.cnt[k] + 1)
        self._mark(tok, r, w)
        return tok

    def dma(self, k, stream, out, in_, r=(), w=(), **kw):
        self._deps(k, r, w)
        stream = ("L%d" % id(w[0])) if len(w) else ("S%d" % id(r[0]))
        key = self.dcur.get(stream)
        if key is None or self.dcnt[key] >= self.DMAX:
            key = None
            while self.dfree:
                cand = self.dfree.pop()
                if self.dcnt[cand] < self.DMAX // 2:
                    key = cand
                    break
            if key is None:
                key = "dq%d" % len(self.dsem)
                self.dsem[key] = self._newsem(key)
                self.dcnt[key] = 0
            self.dcur[stream] = key
        self.eng[k].dma_start(out=out, in_=in_, **kw).then_inc(self.dsem[key], 16)
        self.ninstr += 1
        self.dcnt[key] += 16
        tok = (key, 0, self.dcnt[key])
        self._mark(tok, r, w)
        return tok

    def phase_end(self):
        self.barrier()
        for key in self.dcur.values():
            self.dfree.append(key)
        self.dcur = {}

    def barrier(self):
        toks = []
        for k in self.eng:
            for ep in range(self.epoch[k] + 1):
                n = self.cnt[k] if ep == self.epoch[k] else self.EPOCH
                if n > 0:
                    toks.append((k, ep, n))
        for s in list(self.dsem.keys()):
            if self.dcnt[s] > 0:
                toks.append((s, 0, self.dcnt[s]))
        for k in self.eng:
            for t in toks:
                if t[0] != k:
                    self._wait(k, t)


def host_consts():
    c = {}
    c["ident"] = np.eye(128, dtype=np.float32)
    rr = np.zeros((64, 64), np.float32)
    for i in range(8):
        rr[i + 8, i] = -1.0
        rr[i, i + 8] = 1.0
    c["rrot"] = np.pad(rr, ((0, 64), (0, 0)))
    inv_freq = (500000.0 ** (-np.arange(8, dtype=np.float32) * 2.0 / 16.0)).astype(np.float32)
    f = np.zeros((128, 1), np.float32)
    for d in range(16):
        f[d, 0] = inv_freq[d % 8]
    c["invf"] = f
    k = np.arange(128)[:, None]
    q = np.arange(128)[None, :]
    c["mcur"] = np.tile((k <= q).astype(np.float32), (1, 4))
    c["mprev"] = np.tile((k > q).astype(np.float32), (1, 4))
    u = np.arange(64)[:, None]
    t = np.arange(64)[None, :]
    z = lambda a: np.pad(a.astype(np.float32), ((0, 64), (0, 0)))
    c["mcum"] = z(u <= t)
    c["mmid"] = z((u <= t).astype(np.float32) - (u <= 31).astype(np.float32))
    c["mlc"] = z(u > t)
    c["mones"] = z(np.ones((64, 64)))
    c["caus"] = z(np.tile((u <= t).astype(np.float32), (1, 4)))
    cap = np.zeros((64, 4, 2, 64), np.float32)
    cap[:, :, 0, :] = np.where(u <= t, 0.0, NEG)[:, None, :]
    cap[:, :, 1, :] = np.where(u < t, 0.0, NEG)[:, None, :]
    c["cap"] = z(cap.reshape(64, 512))
    off = {}
    o = 0
    arrs = []
    for n, a in c.items():
        off[n] = (o, a.shape[1])
        o += a.shape[1]
        arrs.append(a)
    return np.ascontiguousarray(np.concatenate(arrs, axis=1)), off


CST, COFF = host_consts()
NCST = CST.shape[1]


def build(dbg=(), nlayers=NL, phases=None):
    nc = bass.Bass("TRN2", target_bir_lowering=False)

    def dram(name, shape, dt, kind="ExternalInput"):
        return nc.dram_tensor(name, shape, dt, kind=kind).ap()

    def scratch(name, shape, dt):
        return dram(name, shape, dt, "ExternalOutput" if name in dbg else "Internal")

    x_d = dram("x", [S_, D_], F32)
    cT_d = dram("cT", [128, 8], F32)
    pos_d = dram("pos", [1, S_], I32)
    adaw_d = dram("ada_w", [NL, D_, 6 * D_], F32)
    adab_d = dram("ada_bT", [NL, 128, 48], F32)
    nmix_d = dram("nmixT", [NL, 128, 8], F32)
    nffn_d = dram("nffnT", [NL, 128, 8], F32)
    win_d = dram("w_in", [NL, D_, DIN], F32)
    wout_d = dram("w_out", [NL, D_, D_], F32)
    wg_d = dram("w_gate", [NL, D_, DFF], F32)
    wu_d = dram("w_up", [NL, D_, DFF], F32)
    wd_d = dram("w_down", [NL, DFF, D_], F32)
    sm_d = dram("small", [NL, 1, 1024], F32)
    cw_d = dram("convT", [NL, 128, 26], F32)
    cst_d = dram("cst", [128, NCST], F32)
    out_d = dram("out", [S_, D_], F32, "ExternalOutput")

    xT_d = scratch("xT", [D_, S_], F32)
    projT_d = scratch("projT", [1408, S_], F32)
    projK_d = scratch("projK", [S_, 1416], F32)
    convK_d = scratch("convK", [S_, 768], F32)
    mixed_d = scratch("mixed", [S_, D_], BF16)

    def run(p):
        return phases is None or p in phases

    with ExitStack() as top:
        K = KB(nc, top)
        K.allbufs = []

        uid = [0]

        def SB(st, name, shape, dt):
            uid[0] += 1
            b = Buf(st.enter_context(nc.sbuf_tensor("%s_%d" % (name, uid[0]), shape, dt)))
            K.allbufs.append(b)
            return b

        def PS(st, name, shape, dt):
            uid[0] += 1
            b = Buf(st.enter_context(nc.psum_tensor("%s_%d" % (name, uid[0]), shape, dt)))
            K.allbufs.append(b)
            return b

        cst = SB(top, "cst_sb", [128, NCST], F32)
        cstb = SB(top, "cstb", [128, NCST], BF16)
        onesb = SB(top, "onesb", [128, 128], BF16)
        modt = SB(top, "modt", [128, 48], F32)
        gscm = SB(top, "gscm", [128, 8], F32)
        gscf = SB(top, "gscf", [128, 8], F32)
        epsb = SB(top, "epsb", [128, 1], F32)
        npib = SB(top, "npib", [128, 1], F32)
        K.dma("sp", "ldc", cst[:, :], cst_d[:, :], w=[cst])
        K.op("dve", lambda e: e.tensor_copy(out=cstb[:, :], in_=cst[:, :]), r=[cst], w=[cstb])
        K.op("dve", lambda e: e.memset(onesb[:, :], 1.0), w=[onesb])
        K.op("dve", lambda e: e.memset(epsb[:, :], EPS), w=[epsb])
        K.op("dve", lambda e: e.memset(npib[:, :], -float(np.pi)), w=[npib])

        def C(name, rows=128, bf=False):
            o, n = COFF[name]
            return (cstb if bf else cst)[0:rows, o:o + n]

        CB = cstb
        CF = cst

        if run("p0"):
            with ExitStack() as st:
                xin = [SB(st, "p0x%d" % i, [128, D_], F32) for i in range(2)]
                xo = [SB(st, "p0o%d" % i, [128, 8, 128], F32) for i in range(2)]
                pt = [PS(st, "p0p%d" % i, [128, 8, 128], F32) for i in range(2)]
                for i in range(S_ // 128):
                    a = xin[i % 2]
                    o = xo[i % 2]
                    p = pt[i % 2]
                    K.dma("sp", "ld", a[:, :], x_d[i * 128:(i + 1) * 128, :], w=[a])
                    for c in range(8):
                        K.op("pe", lambda e: e.transpose(p[:, c, :], a[:, c * 128:(c + 1) * 128], CF[:, COFF["ident"][0]:COFF["ident"][0] + 128]),
                             r=[a, cst], w=[p])
                    K.op("act" if i % 2 else "dve", (lambda e: e.copy(out=o[:, :, :], in_=p[:, :, :])) if i % 2 else
                         (lambda e: e.tensor_copy(out=o[:, :, :], in_=p[:, :, :])), r=[p], w=[o])
                    K.dma("pool", "st", xT_d.rearrange("(c p) t -> p c t", p=128)[:, :, i * 128:(i + 1) * 128], o[:, :, :], r=[o])
            K.phase_end()

        for l in range(nlayers):
            if run("ada"):
                with ExitStack() as st:
                    cnd = SB(st, "a_c", [128, 8], F32)
                    tmp = SB(st, "a_t", [128, 8], F32)
                    ab = SB(st, "a_b", [128, 48], F32)
                    nm = SB(st, "a_nm", [128, 16], F32)
                    wp = [SB(st, "a_w%d" % i, [128, 8, 1024], F32) for i in range(2)]
                    mp = PS(st, "a_ps", [128, 48], F32)
                    K.dma("sp", "ld", cnd[:, :], cT_d[:, :], w=[cnd])
                    K.dma("sp", "ld", ab[:, :], adab_d[l, :, :], w=[ab])
                    K.dma("sp", "ld", nm[:, 0:8], nmix_d[l, :, :], w=[nm])
                    K.dma("sp", "ld", nm[:, 8:16], nffn_d[l, :, :], w=[nm])
                    K.op("act", lambda e: e.activation(out=tmp[:, :], in_=cnd[:, :], func=AF.Exp, scale=-1.0), r=[cnd], w=[tmp])
                    K.op("dve", lambda e: e.tensor_scalar(out=tmp[:, :], in0=tmp[:, :], scalar1=1.0, scalar2=None, op0=ALU.add), r=[tmp], w=[tmp])
                    K.op("dve", lambda e: e.reciprocal(out=tmp[:, :], in_=tmp[:, :]), r=[tmp], w=[tmp])
                    K.op("dve", lambda e: e.tensor_tensor(out=cnd[:, :], in0=cnd[:, :], in1=tmp[:, :], op=ALU.mult), r=[cnd, tmp], w=[cnd])
                    for g in range(6):
                        w = wp[g % 2]
                        for kc in range(8):
                            K.dma("sp", "ld", w[:, kc, :], adaw_d[l, kc * 128:(kc + 1) * 128, g * 1024:(g + 1) * 1024], w=[w])
                        for oc in range(8):
                            for kc in range(8):
                                K.op("pe", lambda e: e.matmul(mp[:, g * 8 + oc:g * 8 + oc + 1], lhsT=w[:, kc, oc * 128:(oc + 1) * 128], rhs=cnd[:, kc:kc + 1],
                                                              start=(kc == 0), stop=(kc == 7)), r=[w, cnd], w=[mp])
                    K.op("dve", lambda e: e.tensor_tensor(out=modt[:, :], in0=mp[:, :], in1=ab[:, :], op=ALU.add), r=[mp, ab], w=[modt])
                    K.op("dve", lambda e: e.scalar_tensor_tensor(out=gscm[:, :], in0=modt[:, 8:16], scalar=1.0, in1=nm[:, 0:8], op0=ALU.add, op1=ALU.mult),
                         r=[modt, nm], w=[gscm])
                    K.op("dve", lambda e: e.scalar_tensor_tensor(out=gscf[:, :], in0=modt[:, 32:40], scalar=1.0, in1=nm[:, 8:16], op0=ALU.add, op1=ALU.mult),
                         r=[modt, nm], w=[gscf])
                K.phase_end()

            def load_weight_bf16(st, name, src_rows, nkc, ncols, dst, piece):
                stg = [SB(st, name + "_s%d" % i, [128, piece], F32) for i in range(6)]
                i = 0
                for kc in range(nkc):
                    for c0 in range(0, ncols, piece):
                        n = min(piece, ncols - c0)
                        s = stg[i % 6]
                        K.dma("sp", "ldw", s[:, 0:n], src_rows(kc)[:, c0:c0 + n], w=[s])
                        eng = ("dve", "pool", "act")[i % 3]
                        if eng == "act":
                            K.op("act", lambda e: e.copy(out=dst[:, kc, c0:c0 + n], in_=s[:, 0:n]), r=[s], w=[dst])
                        else:
                            K.op(eng, lambda e: e.tensor_copy(out=dst[:, kc, c0:c0 + n], in_=s[:, 0:n]), r=[s], w=[dst])
                        i += 1

            def rms_h(st, xt, hT, gsc, shcol, pfx, sq, rstd, tmpf, pss):
                K.op("act", lambda e: e.activation(out=sq[:, :, :], in_=xt[:, :, :], func=AF.Square), r=[xt], w=[sq])
                for c in range(8):
                    K.op("pe", lambda e: e.matmul(pss[:, :], lhsT=onesb[:, :], rhs=sq[:, c, :], start=(c == 0), stop=(c == 7)), r=[onesb, sq], w=[pss])
                K.op("act", lambda e: e.activation(out=rstd[:, :], in_=pss[:, :], func=AF.Sqrt, bias=epsb[:, :], scale=1.0 / D_), r=[pss, epsb], w=[rstd])
                K.op("dve", lambda e: e.reciprocal(out=rstd[:, :], in_=rstd[:, :]), r=[rstd], w=[rstd])
                for c in range(8):
                    t = tmpf[c % 2]
                    K.op("dve", lambda e: e.tensor_tensor(out=t[:, :], in0=xt[:, c, :], in1=rstd[:, :], op=ALU.mult), r=[xt, rstd], w=[t])
                    K.op("act", lambda e: e.activation(out=hT[:, c, :], in_=t[:, :], func=AF.Identity, bias=modt[:, shcol + c:shcol + c + 1],
                                                       scale=gsc[:, c:c + 1]), r=[t, modt, gsc], w=[hT])

            xT_v = xT_d.rearrange("(c p) t -> p c t", p=128)

            if run("m1"):
                with ExitStack() as st:
                    win = SB(st, "m1w", [128, 8, DIN], BF16)
                    load_weight_bf16(st, "m1w", lambda kc: win_d[l, kc * 128:(kc + 1) * 128, :], 8, DIN, win, 706)
                    xt = [SB(st, "m1x%d" % i, [128, 8, 512], F32) for i in range(2)]
                    hT = SB(st, "m1h", [128, 8, 512], BF16)
                    sq = SB(st, "m1sq", [128, 8, 512], BF16)
                    rstd = SB(st, "m1r", [128, 512], F32)
                    tmpf = [SB(st, "m1t%d" % i, [128, 512], F32) for i in range(2)]
                    pss = PS(st, "m1pss", [128, 512], F32)
                    pp = [PS(st, "m1pp%d" % i, [128, 512], F32) for i in range(4)]
                    og = [SB(st, "m1o%d" % i, [128, 512], F32) for i in range(4)]
                    fchunks = [c * 128 for c in range(5)] + [1792 + c * 128 for c in range(6)]
                    kgroups = [(640, 512, 0), (1152, 512, 512), (1664, 128, 1024), (2560, 264, 1152)]
                    n = 0
                    for ti in range(S_ // 512):
                        x_ = xt[ti % 2]
                        K.dma("sp", "ld", x_[:, :, :], xT_v[:, :, ti * 512:(ti + 1) * 512], w=[x_])
                        rms_h(st, x_, hT, gscm, 0, "m1", sq, rstd, tmpf, pss)
                        for fi, c0 in enumerate(fchunks):
                            p = pp[n % 4]
                            o = og[n % 4]
                            for kc in range(8):
                                K.op("pe", lambda e: e.matmul(p[:, :], lhsT=win[:, kc, c0:c0 + 128], rhs=hT[:, kc, :], start=(kc == 0), stop=(kc == 7)),
                                     r=[win, hT], w=[p], inc=(kc == 7))
                            if n % 2:
                                K.op("act", lambda e: e.copy(out=o[:, :], in_=p[:, :]), r=[p], w=[o])
                            else:
                                K.op("dve", lambda e: e.tensor_copy(out=o[:, :], in_=p[:, :]), r=[p], w=[o])
                            K.dma("pool", "st", projT_d[fi * 128:(fi + 1) * 128, ti * 512:(ti + 1) * 512], o[:, :], r=[o])
                            n += 1
                        for sub in range(4):
                            for (c0, nn, d0) in kgroups:
                                p = pp[n % 4]
                                o = og[n % 4]
                                for kc in range(8):
                                    K.op("pe", lambda e: e.matmul(p[:, 0:nn], lhsT=hT[:, kc, sub * 128:(sub + 1) * 128], rhs=win[:, kc, c0:c0 + nn],
                                                                  start=(kc == 0), stop=(kc == 7)), r=[win, hT], w=[p], inc=(kc == 7))
                                if n % 2:
                                    K.op("act", lambda e: e.copy(out=o[:, 0:nn], in_=p[:, 0:nn]), r=[p], w=[o])
                                else:
                                    K.op("dve", lambda e: e.tensor_copy(out=o[:, 0:nn], in_=p[:, 0:nn]), r=[p], w=[o])
                                r0 = ti * 512 + sub * 128
                                K.dma("pool", "st", projK_d[r0:r0 + 128, d0:d0 + nn], o[:, 0:nn], r=[o])
                                n += 1
                K.phase_end()

            sm = sm_d[l]

            if run("m2"):
                with ExitStack() as st:
                    cosF = SB(st, "cosF", [64, S_], F32)
                    sinF = SB(st, "sinF", [64, S_], F32)
                    st0 = st
                    pi_ = SB(st0, "rp_i", [64, S_], I32)
                    u = SB(st0, "rp_u", [64, S_], F32)
                    ki = SB(st0, "rp_k", [64, S_], I32)
                    kf = SB(st0, "rp_kf", [64, S_], F32)
                    m = SB(st0, "rp_m", [64, S_], F32)
                    K.dma("sp", "ld", pi_[:, :], pos_d[0:1, :].to_broadcast([64, S_]), w=[pi_])
                    K.op("dve", lambda e: e.tensor_copy(out=u[:, :], in_=pi_[:, :]), r=[pi_], w=[u])
                    o_if = COFF["invf"][0]
                    K.op("dve", lambda e: e.tensor_scalar(out=u[:, :], in0=u[:, :], scalar1=CF[0:64, o_if:o_if + 1], scalar2=float(1.0 / (2 * np.pi)),
                                                          op0=ALU.mult, op1=ALU.mult), r=[u, cst], w=[u])
                    for (dst, shift) in ((sinF, 0.5), (cosF, 0.75)):
                        K.op("dve", lambda e: e.tensor_scalar(out=m[:, :], in0=u[:, :], scalar1=float(shift), scalar2=None, op0=ALU.add), r=[u], w=[m])
                        K.op("dve", lambda e: e.tensor_copy(out=ki[:, :], in_=m[:, :]), r=[m], w=[ki])
                        K.op("dve", lambda e: e.tensor_copy(out=kf[:, :], in_=ki[:, :]), r=[ki], w=[kf])
                        K.op("dve", lambda e: e.tensor_tensor(out=m[:, :], in0=m[:, :], in1=kf[:, :], op=ALU.subtract), r=[m, kf], w=[m])
                        K.op("dve", lambda e: e.tensor_scalar(out=kf[:, :], in0=m[:, :], scalar1=0.0, scalar2=None, op0=ALU.is_lt), r=[m], w=[kf])
                        K.op("dve", lambda e: e.tensor_tensor(out=m[:, :], in0=m[:, :], in1=kf[:, :], op=ALU.add), r=[m, kf], w=[m])
                        K.op("dve", lambda e: e.tensor_scalar(out=kf[:, :], in0=m[:, :], scalar1=1.0, scalar2=None, op0=ALU.is_ge), r=[m], w=[kf])
                        K.op("dve", lambda e: e.tensor_tensor(out=m[:, :], in0=m[:, :], in1=kf[:, :], op=ALU.subtract), r=[m, kf], w=[m])
                        K.op("act", lambda e: e.activation(out=dst[:, :], in_=m[:, :], func=AF.Sin, bias=npib[0:64, :], scale=float(2 * np.pi)),
                             r=[m, npib], w=[dst])
                    gq = SB(st, "m2g", [64, 2], F32)
                    es = SB(st, "m2es", [128, 8], F32)
                    K.dma("sp", "ld", gq[:, 0:2], cw_d[l, 0:64, 24:26], w=[gq])
                    K.dma("sp", "ld", es[:, :], sm[0:1, 128:136].to_broadcast([128, 8]), w=[es])
                    K.op("act", lambda e: e.activation(out=es[:, :], in_=es[:, :], func=AF.Exp), r=[es], w=[es])
                    raw = [SB(st, "m2raw%d" % i, [64, 10, 128], F32) for i in range(2)]
                    sqb = SB(st, "m2sq", [64, 10, 128], BF16)
                    rs = SB(st, "m2rs", [64, 10, 128], F32)
                    qn = SB(st, "m2qn", [64, 10, 128], F32)
                    qnb = SB(st, "m2qnb", [64, 10, 128], BF16)
                    t1 = SB(st, "m2t1", [64, 10, 128], F32)
                    t2 = SB(st, "m2t2", [64, 10, 128], F32)
                    qk = [SB(st, "m2qk%d" % i, [64, 10, 128], BF16) for i in range(2)]
                    vraw = [SB(st, "m2vr%d" % i, [128, 128], F32) for i in range(2)]
                    vb = [SB(st, "m2vb%d" % i, [128, 2, 65], BF16) for i in range(2)]
                    pT = [SB(st, "m2pT%d" % i, [128, 512], BF16) for i in range(4)]
                    den = SB(st, "m2den", [128, 8], F32)
                    oo = [SB(st, "m2oo%d" % i, [128, 8, 64], BF16) for i in range(2)]
                    ps3 = PS(st, "m2ps3", [64, 3, 512], F32)
                    psS = [PS(st, "m2pS%d" % i, [128, 512], F32) for i in range(2)]
                    psO = PS(st, "m2pO", [128, 8, 128], F32)
                    for b in vb:
                        K.op("dve", lambda e: e.memset(b[:, :, 64:65], 1.0), w=[b])
                    o_rr = COFF["rrot"][0]
                    o_mc = COFF["mcur"][0]
                    o_mp = COFF["mprev"][0]
                    nb = S_ // 128
                    for i in range(nb):
                        rw = raw[i % 2]
                        cur = qk[i % 2]
                        prv = qk[(i + 1) % 2]
                        K.dma("sp", "ld", rw[:, 0:8, :], projT_d[0:512, i * 128:(i + 1) * 128].rearrange("(h d) t -> d h t", d=64), w=[rw])
                        K.dma("sp", "ld", rw[:, 8:10, :], projT_d[512:640, i * 128:(i + 1) * 128].rearrange("(h d) t -> d h t", d=64), w=[rw])
                        vr = vraw[i % 2]
                        K.dma("sp", "ld", vr[:, :], projK_d[i * 128:(i + 1) * 128, 0:128], w=[vr])
                        vcur = vb[i % 2]
                        vprv = vb[(i + 1) % 2]
                        K.op("pool", lambda e: e.tensor_copy(out=vcur[:, :, 0:64], in_=vr[:, :].rearrange("p (g d) -> p g d", d=64)), r=[vr], w=[vcur])
                        K.op("act", lambda e: e.activation(out=sqb[:, :, :], in_=rw[:, :, :], func=AF.Square), r=[rw], w=[sqb])
                        for j, (a, b_) in enumerate(((0, 4), (4, 8), (8, 10))):
                            K.op("pe", lambda e: e.matmul(ps3[:, j, 0:(b_ - a) * 128], lhsT=onesb[0:64, 0:64], rhs=sqb[:, a:b_, :], start=True, stop=True),
                                 r=[onesb, sqb], w=[ps3])
                        K.op("act", lambda e: e.activation(out=rs[:, :, :].rearrange("p h t -> p (h t)"), in_=ps3[:, :, :].rearrange("p a n -> p (a n)")[:, 0:1280],
                                                           func=AF.Ln, bias=epsb[0:64, :], scale=1.0 / 64), r=[ps3, epsb], w=[rs])
                        K.op("act", lambda e: e.activation(out=rs[:, :, :], in_=rs[:, :, :], func=AF.Exp, scale=-0.5), r=[rs], w=[rs])
                        K.op("dve", lambda e: e.scalar_tensor_tensor(out=qn[:, 0:8, :], in0=rw[:, 0:8, :], scalar=gq[:, 0:1], in1=rs[:, 0:8, :], op0=ALU.mult, op1=ALU.mult),
                             r=[rw, gq, rs], w=[qn])
                        K.op("dve", lambda e: e.scalar_tensor_tensor(out=qn[:, 8:10, :], in0=rw[:, 8:10, :], scalar=gq[:, 1:2], in1=rs[:, 8:10, :], op0=ALU.mult, op1=ALU.mult),
                             r=[rw, gq, rs], w=[qn])
                        K.op("act", lambda e: e.copy(out=qnb[:, :, :], in_=qn[:, :, :]), r=[qn], w=[qnb])
                        for j, (a, b_) in enumerate(((0, 4), (4, 8), (8, 10))):
                            K.op("pe", lambda e: e.matmul(ps3[:, j, 0:(b_ - a) * 128], lhsT=CB[0:64, o_rr:o_rr + 64], rhs=qnb[:, a:b_, :], start=True, stop=True),
                                 r=[cstb, qnb], w=[ps3])
                        cs = cosF[:, i * 128:(i + 1) * 128]
                        sn = sinF[:, i * 128:(i + 1) * 128]
                        K.op("dve", lambda e: e.tensor_tensor(out=t1[:, :, :], in0=qn[:, :, :], in1=cs.rearrange("p (o t) -> p o t", o=1).to_broadcast([64, 10, 128]), op=ALU.mult),
                             r=[qn, cosF], w=[t1])
                        K.op("dve", lambda e: e.tensor_tensor(out=t2[:, :, :], in0=ps3[:, :, :].rearrange("p a n -> p (a n)")[:, 0:1280].rearrange("p (h t) -> p h t", t=128),
                                                              in1=sn.rearrange("p (o t) -> p o t", o=1).to_broadcast([64, 10, 128]), op=ALU.mult), r=[ps3, sinF], w=[t2])
                        K.op("dve", lambda e: e.tensor_tensor(out=cur[:, :, :], in0=t1[:, :, :], in1=t2[:, :, :], op=ALU.add), r=[t1, t2], w=[cur])
                        for g in range(2):
                            pc = pT[2 * g]
                            pp_ = pT[2 * g + 1]
                            K.op("pe", lambda e: e.matmul(psS[0][:, :], lhsT=cur[:, 8 + g, :], rhs=cur[:, 4 * g:4 * g + 4, :], start=True, stop=True), r=[cur], w=[psS[0]])
                            K.op("act", lambda e: e.activation(out=pc[:, :], in_=psS[0][:, :], func=AF.Exp, scale=0.125), r=[psS[0]], w=[pc])
                            K.op("dve", lambda e: e.tensor_tensor(out=pc[:, :], in0=pc[:, :], in1=CB[:, o_mc:o_mc + 512], op=ALU.mult), r=[pc, cstb], w=[pc])
                            if i > 0:
                                K.op("pe", lambda e: e.matmul(psS[1][:, :], lhsT=prv[:, 8 + g, :], rhs=cur[:, 4 * g:4 * g + 4, :], start=True, stop=True), r=[cur, prv], w=[psS[1]])
                                K.op("act", lambda e: e.activation(out=pp_[:, :], in_=psS[1][:, :], func=AF.Exp, scale=0.125), r=[psS[1]], w=[pp_])
                                K.op("pool", lambda e: e.tensor_tensor(out=pp_[:, :], in0=pp_[:, :], in1=CB[:, o_mp:o_mp + 512], op=ALU.mult), r=[pp_, cstb], w=[pp_])
                            for j in range(4):
                                h = 4 * g + j
                                K.op("pe", lambda e: e.matmul(psO[:, h, 0:65], lhsT=pc[:, j * 128:(j + 1) * 128], rhs=vcur[:, g, :], start=True, stop=(i == 0)),
                                     r=[pc, vcur], w=[psO])
                                if i > 0:
                                    K.op("pe", lambda e: e.matmul(psO[:, h, 0:65], lhsT=pp_[:, j * 128:(j + 1) * 128], rhs=vprv[:, g, :], start=False, stop=True),
                                         r=[pp_, vprv], w=[psO])
                        K.op("dve", lambda e: e.tensor_tensor(out=den[:, :], in0=psO[:, :, 64], in1=es[:, :], op=ALU.add), r=[psO, es], w=[den])
                        K.op("dve", lambda e: e.reciprocal(out=den[:, :], in_=den[:, :]), r=[den], w=[den])
                        o_ = oo[i % 2]
                        K.op("dve", lambda e: e.tensor_tensor(out=o_[:, :, :], in0=psO[:, :, 0:64], in1=den[:, :].rearrange("p (h o) -> p h o", o=1).to_broadcast([128, 8, 64]),
                                                              op=ALU.mult), r=[psO, den], w=[o_])
                        K.dma("pool", "st", mixed_d[i * 128:(i + 1) * 128, 0:512], o_[:, :, :].rearrange("p h d -> p (h d)"), r=[o_])
                K.phase_end()

            def out_norm_gate(o_ps, gate_src, gnb, dst_cols, r0, tmpA, tmpB, ssq, ob, extra_r):
                K.op("act", lambda e: e.activation(out=tmpA[:, :].rearrange("p (h d) -> p h d", d=64), in_=o_ps[:, 0:4, :], func=AF.Square), r=[o_ps], w=[tmpA])
                K.op("dve", lambda e: e.tensor_reduce(out=ssq[:, :], in_=tmpA[:, :].rearrange("p (h d) -> p h d", d=64), axis=AX.X, op=ALU.add), r=[tmpA], w=[ssq])
                K.op("act", lambda e: e.activation(out=ssq[:, :], in_=ssq[:, :], func=AF.Ln, bias=epsb[0:64, :], scale=1.0 / 64), r=[ssq, epsb], w=[ssq])
                K.op("act", lambda e: e.activation(out=ssq[:, :], in_=ssq[:, :], func=AF.Exp, scale=-0.5), r=[ssq], w=[ssq])
                K.op("dve", lambda e: e.tensor_tensor(out=tmpA[:, :].rearrange("p (h d) -> p h d", d=64), in0=o_ps[:, 0:4, :],
                                                      in1=ssq[:, :].rearrange("p (h o) -> p h o", o=1).to_broadcast([64, 4, 64]), op=ALU.mult), r=[o_ps, ssq], w=[tmpA])
                K.op("dve", lambda e: e.tensor_tensor(out=tmpA[:, :], in0=tmpA[:, :], in1=gnb[:, :], op=ALU.mult), r=[tmpA, gnb], w=[tmpA])
                K.op("act", lambda e: e.activation(out=tmpB[:, :], in_=gate_src, func=AF.Exp, scale=-1.0), r=extra_r, w=[tmpB])
                K.op("dve", lambda e: e.tensor_scalar(out=tmpB[:, :], in0=tmpB[:, :], scalar1=1.0, scalar2=None, op0=ALU.add), r=[tmpB], w=[tmpB])
                K.op("dve", lambda e: e.reciprocal(out=tmpB[:, :], in_=tmpB[:, :]), r=[tmpB], w=[tmpB])
                K.op("dve", lambda e: e.tensor_tensor(out=tmpB[:, :], in0=tmpB[:, :], in1=gate_src, op=ALU.mult), r=[tmpB] + list(extra_r), w=[tmpB])
                K.op("dve", lambda e: e.tensor_tensor(out=ob[:, :], in0=tmpA[:, :], in1=tmpB[:, :], op=ALU.mult), r=[tmpA, tmpB], w=[ob])
                K.dma("pool", "st", mixed_d[r0:r0 + 64, dst_cols[0]:dst_cols[1]], ob[:, :], r=[ob])

            o_id = COFF["ident"][0]
            o_cum = COFF["mcum"][0]
            o_mid = COFF["mmid"][0]
            o_lc = COFF["mlc"][0]
            o_on = COFF["mones"][0]
            o_ca = COFF["caus"][0]
            o_cap = COFF["cap"][0]

            if run("m3"):
                NB3 = 2
                with ExitStack() as st:
                    lbb = SB(st, "m3lb", [64, 256], F32)
                    oml = SB(st, "m3oml", [64, 256], F32)
                    gnb = SB(st, "m3gn", [64, 256], F32)
                    if l == 0:
                        K.op("dve", lambda e: e.memset(lbb[:, :], 0.0), w=[lbb])
                    else:
                        t0 = SB(st, "m3lt", [64, 256], F32)
                        K.dma("sp", "ld", lbb[:, :], sm_d[0, 0:1, 136:392].to_broadcast([64, 256]), w=[lbb])
                        K.dma("sp", "ld", t0[:, :], sm_d[1, 0:1, 136:392].to_broadcast([64, 256]), w=[t0])
                        K.op("dve", lambda e: e.tensor_tensor(out=lbb[:, :], in0=lbb[:, :], in1=t0[:, :], op=ALU.subtract), r=[lbb, t0], w=[lbb])
                        K.op("act", lambda e: e.activation(out=lbb[:, :], in_=lbb[:, :], func=AF.Exp), r=[lbb], w=[lbb])
                        K.op("dve", lambda e: e.tensor_scalar(out=lbb[:, :], in0=lbb[:, :], scalar1=1.0, scalar2=None, op0=ALU.add), r=[lbb], w=[lbb])
                        K.op("dve", lambda e: e.reciprocal(out=lbb[:, :], in_=lbb[:, :]), r=[lbb], w=[lbb])
                    K.op("dve", lambda e: e.tensor_scalar(out=oml[:, :], in0=lbb[:, :], scalar1=-1.0, scalar2=1.0, op0=ALU.mult, op1=ALU.add), r=[lbb], w=[oml])
                    for h in range(4):
                        K.dma("sp", "ld", gnb[:, h * 64:(h + 1) * 64], sm[0:1, 648:712].to_broadcast([64, 64]), w=[gnb])
                    NB = NB3
                    inp = [SB(st, "m3in%d" % i, [64, NB, 1024], F32) for i in range(2)]
                    e_ = SB(st, "m3e", [64, NB, 256], F32)
                    s_ = SB(st, "m3s", [64, NB, 256], F32)
                    f_ = SB(st, "m3f", [64, NB, 256], F32)
                    lf = SB(st, "m3lf", [64, NB, 256], F32)
                    kk = SB(st, "m3kk", [64, NB, 256], F32)
                    ex = SB(st, "m3ex", [64, NB, 4, 256], F32)
                    qq = SB(st, "m3qq", [64, NB, 4, 256], F32)
                    trs = SB(st, "m3tr", [64, NB, 12, 64], F32)
                    att = SB(st, "m3att", [64, NB, 256], F32)
                    el = SB(st, "m3el", [64, NB, 4], F32)
                    Sf = SB(st, "m3S", [64, 4, 64], F32)
                    tA = SB(st, "m3tA", [64, NB, 256], F32)
                    tB = SB(st, "m3tB", [64, NB, 256], F32)
                    ssq = SB(st, "m3ssq", [64, NB, 4], F32)
                    ob = [SB(st, "m3ob%d" % i, [64, NB, 256], BF16) for i in range(2)]
                    pc = PS(st, "m3pc", [64, NB, 512], F32)
                    pB = PS(st, "m3pB", [64, 512], F32)
                    ptr = PS(st, "m3ptr", [64, 3, 512], F32)
                    pD_ = PS(st, "m3pD", [64, 512], F32)
                    pE = PS(st, "m3pE", [64, 512], F32)
                    K.op("dve", lambda e: e.memset(Sf[:, :, :], 0.0), w=[Sf])
                    K.op("dve", lambda e: e.memset(att[:, :, :], 0.0), w=[att])
                    identf3 = CF[0:64, o_id:o_id + 64]
                    ptrf = ptr[:, :, :].rearrange("p b x -> p (b x)")
                    msk3 = CF[0:64, o_ca:o_ca + 256].rearrange("p (h t) -> p h t", t=64)
                    bcn = lambda t2: t2.rearrange("p (o x) -> p o x", o=1).to_broadcast([64, NB, 256])
                    for gi_ in range(S_ // (64 * NB)):
                        r0 = gi_ * 64 * NB
                        x_ = inp[gi_ % 2]
                        K.dma("sp", "ld", x_[:, :, :], projK_d[r0:r0 + 64 * NB, 128:1152].rearrange("(n p) c -> p n c", p=64), w=[x_])
                        K.op("act", lambda e: e.activation(out=e_[:, :, :], in_=x_[:, :, 256:512], func=AF.Exp, scale=-1.0), r=[x_], w=[e_])
                        K.op("dve", lambda e: e.tensor_scalar(out=s_[:, :, :], in0=e_[:, :, :], scalar1=1.0, scalar2=None, op0=ALU.add), r=[e_], w=[s_])
                        K.op("dve", lambda e: e.reciprocal(out=s_[:, :, :], in_=s_[:, :, :]), r=[s_], w=[s_])
                        K.op("dve", lambda e: e.tensor_tensor(out=s_[:, :, :], in0=s_[:, :, :], in1=bcn(oml[:, :]), op=ALU.mult), r=[s_, oml], w=[s_])
                        K.op("dve", lambda e: e.tensor_tensor(out=f_[:, :, :], in0=s_[:, :, :], in1=bcn(lbb[:, :]), op=ALU.add), r=[s_, lbb], w=[f_])
                        K.op("act", lambda e: e.activation(out=lf[:, :, :], in_=f_[:, :, :], func=AF.Ln), r=[f_], w=[lf])
                        K.op("dve", lambda e: e.tensor_tensor(out=kk[:, :, :], in0=s_[:, :, :], in1=e_[:, :, :], op=ALU.mult), r=[s_, e_], w=[kk])
                        for n in range(NB):
                            K.op("pe", lambda e: e.matmul(pc[:, n, 0:256], lhsT=CF[0:64, o_cum:o_cum + 64], rhs=lf[:, n, :], start=True, stop=True), r=[cst, lf], w=[pc])
                            K.op("pe", lambda e: e.matmul(pc[:, n, 256:512], lhsT=CF[0:64, o_mid:o_mid + 64], rhs=lf[:, n, :], start=True, stop=True), r=[cst, lf], w=[pc])
                            K.op("pe", lambda e: e.matmul(pB[:, n * 256:(n + 1) * 256], lhsT=CF[0:64, o_lc:o_lc + 64], rhs=lf[:, n, :], start=True, stop=True), r=[cst, lf], w=[pB])
                            for h in range(4):
                                K.op("pe", lambda e: e.matmul(pD_[:, n * 4 + h:n * 4 + h + 1], lhsT=lf[:, n, h * 64:(h + 1) * 64], rhs=CF[0:64, o_on:o_on + 1], start=True, stop=True),
                                     r=[cst, lf], w=[pD_])
                        K.op("act", lambda e: e.activation(out=ex[:, :, 0:2, :].rearrange("p n a x -> p n (a x)"), in_=pc[:, :, :], func=AF.Exp), r=[pc], w=[ex])
                        K.op("act", lambda e: e.activation(out=ex[:, :, 2, :], in_=pc[:, :, 256:512], func=AF.Exp, scale=-1.0), r=[pc], w=[ex])
                        K.op("act", lambda e: e.activation(out=ex[:, :, 3, :], in_=pB[:, :].rearrange("p (n x) -> p n x", n=NB), func=AF.Exp), r=[pB], w=[ex])
                        K.op("act", lambda e: e.activation(out=el[:, :, :].rearrange("p n h -> p (n h)"), in_=pD_[:, 0:NB * 4], func=AF.Exp), r=[pD_], w=[el])
                        for n in range(NB):
                            K.op("dve", lambda e: e.tensor_tensor(out=qq[:, n, 0:2, :], in0=ex[:, n, 0:2, :], in1=x_[:, n, 0:256].rearrange("p (o x) -> p o x", o=1).to_broadcast([64, 2, 256]),
                                                                  op=ALU.mult), r=[ex, x_], w=[qq])
                            K.op("dve", lambda e: e.tensor_tensor(out=qq[:, n, 2:4, :], in0=ex[:, n, 2:4, :], in1=kk[:, n, :].rearrange("p (o x) -> p o x", o=1).to_broadcast([64, 2, 256]),
                                                                  op=ALU.mult), r=[ex, kk], w=[qq])
                        for n in range(NB):
                            for a in range(3):
                                for h in range(4):
                                    sl_ = n * 12 + a * 4 + h
                                    K.op("pe", lambda e: e.transpose(ptrf[:, sl_ * 64:(sl_ + 1) * 64], qq[:, n, a, h * 64:(h + 1) * 64], identf3), r=[qq, cst], w=[ptr])
                        K.op("act", lambda e: e.copy(out=trs[:, :, :, :].rearrange("p n s t -> p (n s t)"), in_=ptrf[:, 0:NB * 768]), r=[ptr], w=[trs])
                        for n in range(NB):
                            for h in range(4):
                                c0 = n * 256 + h * 64
                                K.op("pe", lambda e: e.matmul(pc[0:32, 0, c0:c0 + 32], lhsT=trs[:, n, 8 + h, 0:32], rhs=trs[:, n, 4 + h, 0:32], start=True, stop=True), r=[trs], w=[pc])
                                K.op("pe", lambda e: e.matmul(pc[:, 0, c0 + 32:c0 + 64], lhsT=trs[:, n, 8 + h, :], rhs=trs[:, n, 4 + h, 32:64], start=True, stop=True), r=[trs], w=[pc])
                        for n in range(NB):
                            att3 = att[:, n, :].rearrange("p (h t) -> p h t", t=64)
                            pat3 = pc[:, 0, n * 256:(n + 1) * 256].rearrange("p (h t) -> p h t", t=64)
                            K.op("dve", lambda e: e.tensor_tensor(out=att3[0:32, :, 0:32], in0=pat3[0:32, :, 0:32], in1=msk3[0:32, :, 0:32], op=ALU.mult), r=[pc, cst], w=[att])
                            K.op("dve", lambda e: e.tensor_tensor(out=att3[:, :, 32:64], in0=pat3[:, :, 32:64], in1=msk3[:, :, 32:64], op=ALU.mult), r=[pc, cst], w=[att])
                        for n in range(NB):
                            for h in range(4):
                                c0 = n * 256 + h * 64
                                K.op("pe", lambda e: e.matmul(pE[:, c0:c0 + 64], lhsT=att[:, n, h * 64:(h + 1) * 64], rhs=x_[:, n, 512 + h * 64:512 + (h + 1) * 64], start=True, stop=False),
                                     r=[att, x_], w=[pE])
                                K.op("pe", lambda e: e.matmul(pE[:, c0:c0 + 64], lhsT=trs[:, n, h, :], rhs=Sf[:, h, :], start=False, stop=True), r=[trs, Sf], w=[pE])
                            for h in range(4):
                                K.op("pe", lambda e: e.matmul(pc[:, 1, h * 64:(h + 1) * 64], lhsT=qq[:, n, 3, h * 64:(h + 1) * 64], rhs=x_[:, n, 512 + h * 64:512 + (h + 1) * 64], start=True, stop=True),
                                     r=[qq, x_], w=[pc])
                            K.op("dve", lambda e: e.tensor_tensor(out=Sf[:, :, :], in0=Sf[:, :, :], in1=el[:, n, :].rearrange("p (h o) -> p h o", o=1).to_broadcast([64, 4, 64]),
                                                                  op=ALU.mult), r=[Sf, el], w=[Sf])
                            K.op("dve", lambda e: e.tensor_tensor(out=Sf[:, :, :], in0=Sf[:, :, :], in1=pc[:, 1, 0:256].rearrange("p (h d) -> p h d", d=64), op=ALU.add), r=[Sf, pc], w=[Sf])
                        o3 = pE[:, 0:NB * 256].rearrange("p (m d) -> p m d", d=64)
                        tA3 = tA[:, :, :].rearrange("p n (h d) -> p (n h) d", d=64)
                        K.op("act", lambda e: e.activation(out=tA3, in_=o3, func=AF.Square), r=[pE], w=[tA])
                        K.op("dve", lambda e: e.tensor_reduce(out=ssq[:, :, :].rearrange("p n h -> p (n h)"), in_=tA3, axis=AX.X, op=ALU.add), r=[tA], w=[ssq])
                        K.op("act", lambda e: e.activation(out=ssq[:, :, :], in_=ssq[:, :, :], func=AF.Ln, bias=epsb[0:64, :], scale=1.0 / 64), r=[ssq, epsb], w=[ssq])
                        K.op("act", lambda e: e.activation(out=ssq[:, :, :], in_=ssq[:, :, :], func=AF.Exp, scale=-0.5), r=[ssq], w=[ssq])
                        K.op("dve", lambda e: e.tensor_tensor(out=tA3, in0=o3, in1=ssq[:, :, :].rearrange("p n (h o) -> p (n h) o", o=1).to_broadcast([64, NB * 4, 64]), op=ALU.mult),
                             r=[pE, ssq], w=[tA])
                        K.op("dve", lambda e: e.tensor_tensor(out=tA[:, :, :], in0=tA[:, :, :], in1=bcn(gnb[:, :]), op=ALU.mult), r=[tA, gnb], w=[tA])
                        K.op("act", lambda e: e.activation(out=tB[:, :, :], in_=x_[:, :, 768:1024], func=AF.Exp, scale=-1.0), r=[x_], w=[tB])
                        K.op("dve", lambda e: e.tensor_scalar(out=tB[:, :, :], in0=tB[:, :, :], scalar1=1.0, scalar2=None, op0=ALU.add), r=[tB], w=[tB])
                        K.op("dve", lambda e: e.reciprocal(out=tB[:, :, :], in_=tB[:, :, :]), r=[tB], w=[tB])
                        K.op("dve", lambda e: e.tensor_tensor(out=tB[:, :, :], in0=tB[:, :, :], in1=x_[:, :, 768:1024], op=ALU.mult), r=[tB, x_], w=[tB])
                        o_ = ob[gi_ % 2]
                        K.op("dve", lambda e: e.tensor_tensor(out=o_[:, :, :], in0=tA[:, :, :], in1=tB[:, :, :], op=ALU.mult), r=[tA, tB], w=[o_])
                        K.dma("pool", "st", mixed_d[r0:r0 + 64 * NB, 512:768].rearrange("(n p) c -> p n c", p=64), o_[:, :, :], r=[o_])
                    K.phase_end()

            if run("m4a"):
                with ExitStack() as st:
                    cw = SB(st, "m4cw", [128, 24], F32)
                    K.dma("sp", "ld", cw[:, :], cw_d[l, :, 0:24], w=[cw])
                    xin = [SB(st, "m4x%d" % i, [128, 515], F32) for i in range(2)]
                    accs = [SB(st, "m4acc%d" % i, [128, 512], F32) for i in range(2)]
                    sgs = [SB(st, "m4sg%d" % i, [128, 512], F32) for i in range(2)]
                    oo = [SB(st, "m4o%d" % i, [128, 4, 128], F32) for i in range(2)]
                    pt = [PS(st, "m4p%d" % i, [128, 4, 128], F32) for i in range(2)]
                    n = 0
                    for c in range(6):
                        for ti in range(S_ // 512):
                            x_ = xin[n % 2]
                            acc = accs[n % 2]
                            sg = sgs[n % 2]
                            rows = projT_d[640 + c * 128:640 + (c + 1) * 128, :]
                            if ti == 0:
                                K.op("dve", lambda e: e.memset(x_[:, 0:3], 0.0), w=[x_])
                                K.dma("sp", "ld", x_[:, 3:515], rows[:, 0:512], w=[x_])
                            else:
                                K.dma("sp", "ld", x_[:, 0:515], rows[:, ti * 512 - 3:(ti + 1) * 512], w=[x_])
                            K.op("dve", lambda e: e.tensor_scalar(out=acc[:, :], in0=x_[:, 0:512], scalar1=cw[:, c * 4:c * 4 + 1], scalar2=None, op0=ALU.mult), r=[x_, cw], w=[acc])
                            for j in range(1, 4):
                                K.op("dve", lambda e: e.scalar_tensor_tensor(out=acc[:, :], in0=x_[:, j:j + 512], scalar=cw[:, c * 4 + j:c * 4 + j + 1], in1=acc[:, :],
                                                                             op0=ALU.mult, op1=ALU.add), r=[x_, cw, acc], w=[acc])
                            K.op("act", lambda e: e.activation(out=sg[:, :], in_=acc[:, :], func=AF.Exp, scale=-1.0), r=[acc], w=[sg])
                            K.op("dve", lambda e: e.tensor_scalar(out=sg[:, :], in0=sg[:, :], scalar1=1.0, scalar2=None, op0=ALU.add), r=[sg], w=[sg])
                            K.op("dve", lambda e: e.reciprocal(out=sg[:, :], in_=sg[:, :]), r=[sg], w=[sg])
                            K.op("dve", lambda e: e.tensor_tensor(out=acc[:, :], in0=acc[:, :], in1=sg[:, :], op=ALU.mult), r=[acc, sg], w=[acc])
                            p = pt[n % 2]
                            o = oo[n % 2]
                            for s4 in range(4):
                                K.op("pe", lambda e: e.transpose(p[:, s4, :], acc[:, s4 * 128:(s4 + 1) * 128], CF[:, o_id:o_id + 128]), r=[acc, cst], w=[p])
                            K.op("act", lambda e: e.copy(out=o[:, :, :], in_=p[:, :, :]), r=[p], w=[o])
                            K.dma("pool", "st", convK_d[ti * 512:(ti + 1) * 512, c * 128:(c + 1) * 128].rearrange("(s p) c -> p s c", p=128), o[:, :, :], r=[o])
                            n += 1
                K.phase_end()

            if run("m4b"):
                NB = 2
                with ExitStack() as st:
                    nA = SB(st, "g_nA", [64, 4], F32)
                    dtb = SB(st, "g_dtb", [64, 4], F32)
                    gnb = SB(st, "g_gn", [64, 256], F32)
                    K.dma("sp", "ld", nA[:, :], sm[0:1, 712:716].to_broadcast([64, 4]), w=[nA])
                    K.dma("sp", "ld", dtb[:, :], sm[0:1, 716:720].to_broadcast([64, 4]), w=[dtb])
                    for h in range(4):
                        K.dma("sp", "ld", gnb[:, h * 64:(h + 1) * 64], sm[0:1, 720:784].to_broadcast([64, 64]), w=[gnb])
                    K.op("act", lambda e: e.activation(out=nA[:, :], in_=nA[:, :], func=AF.Exp), r=[nA], w=[nA])
                    K.op("dve", lambda e: e.tensor_scalar(out=nA[:, :], in0=nA[:, :], scalar1=-1.0, scalar2=None, op0=ALU.mult), r=[nA], w=[nA])
                    cin = [SB(st, "g_in%d" % i, [64, NB, 768], F32) for i in range(2)]
                    gin = [SB(st, "g_gi%d" % i, [64, NB, 264], F32) for i in range(2)]
                    sq = SB(st, "g_sq", [64, NB, 512], F32)
                    ss = SB(st, "g_ss", [64, NB, 8], F32)
                    qkn = SB(st, "g_qkn", [64, NB, 8, 64], F32)
                    eb = SB(st, "g_eb", [64, NB, 4], F32)
                    lnb = SB(st, "g_lnb", [64, NB, 4], F32)
                    beta = SB(st, "g_beta", [64, NB, 4], F32)
                    g_ = SB(st, "g_g", [64, NB, 4], F32)
                    gcols = SB(st, "g_gc", [64, 2, NB, 4, 64], F32)
                    G = SB(st, "g_G", [64, NB, 4], F32)
                    eG = SB(st, "g_eG", [64, 3, NB, 4], F32)
                    DT = SB(st, "g_DT", [64, NB, 4, 2, 64], F32)
                    LF = SB(st, "g_LF", [64, NB, 4, 2, 64], F32)
                    qdec = SB(st, "g_qdec", [64, NB, 4, 64], F32)
                    tA4 = SB(st, "g_tA4", [64, NB, 4, 64], F32)
                    kbg = SB(st, "g_kbg", [64, NB, 4, 64], F32)
                    kdec = SB(st, "g_kdec", [64, NB, 4, 64], F32)
                    vbt = SB(st, "g_vb", [64, NB, 4, 64], F32)
                    trs = SB(st, "g_tr", [64, NB, 3, 4, 64], F32)
                    qkT = SB(st, "g_qkT", [64, NB, 4, 2, 64], F32)
                    Mx = SB(st, "g_M", [64, NB, 4, 2, 64], F32)
                    QKm = SB(st, "g_QKm", [64, NB, 4, 64], F32)
                    XX = [SB(st, "g_XX%d" % i, [64, 2, NB, 4, 64], F32) for i in range(2)]
                    Tm = SB(st, "g_Tm", [64, NB, 4, 64], F32)
                    nWT = SB(st, "g_nWT", [64, NB, 4, 64], F32)
                    vnew = SB(st, "g_vn", [64, 4, 64], F32)
                    Sf = SB(st, "g_S", [64, 4, 64], F32)
                    tA = SB(st, "g_tA", [64, NB, 256], F32)
                    tB = SB(st, "g_tB", [64, NB, 256], F32)
                    ssq = SB(st, "g_ssq", [64, NB, 4], F32)
                    ob = [SB(st, "g_ob%d" % i, [64, NB, 256], BF16) for i in range(2)]
                    bD = PS(st, "g_bD", [64, NB, 512], F32)
                    bT = PS(st, "g_bT", [64, 3, 512], F32)
                    bX = PS(st, "g_bX", [64, 2, 512], F32)
                    bU = PS(st, "g_bU", [64, 512], F32)
                    K.op("dve", lambda e: e.memset(Sf[:, :, :], 0.0), w=[Sf])
                    identf = CF[0:64, o_id:o_id + 64]
                    bD5 = bD[:, :, :].rearrange("p n (h a t) -> p n h a t", h=4, a=2)
                    bTf = bT[:, :, :].rearrange("p b x -> p (b x)")
                    ng = S_ // (64 * NB)
                    for gi_ in range(ng):
                        r0 = gi_ * 64 * NB
                        x_ = cin[gi_ % 2]
                        gi = gin[gi_ % 2]
                        K.dma("sp", "ld", x_[:, :, :], convK_d[r0:r0 + 64 * NB, :].rearrange("(n p) c -> p n c", p=64), w=[x_])
                        K.dma("sp", "ld", gi[:, :, :], projK_d[r0:r0 + 64 * NB, 1152:1416].rearrange("(n p) c -> p n c", p=64), w=[gi])
                        xqk = x_[:, :, 0:512].rearrange("p n (h d) -> p n h d", d=64)
                        xv = x_[:, :, 512:768].rearrange("p n (h d) -> p n h d", d=64)
                        K.op("act", lambda e: e.activation(out=sq[:, :, :], in_=x_[:, :, 0:512], func=AF.Square), r=[x_], w=[sq])
                        K.op("dve", lambda e: e.tensor_reduce(out=ss[:, :, :].rearrange("p n h -> p (n h)"), in_=sq[:, :, :].rearrange("p n (h d) -> p (n h) d", d=64),
                                                              axis=AX.X, op=ALU.add), r=[sq], w=[ss])
                        K.op("act", lambda e: e.activation(out=ss[:, :, :], in_=ss[:, :, :], func=AF.Ln, bias=epsb[0:64, :], scale=1.0), r=[ss, epsb], w=[ss])
                        K.op("act", lambda e: e.activation(out=ss[:, :, :], in_=ss[:, :, :], func=AF.Exp, scale=-0.5), r=[ss], w=[ss])
                        K.op("dve", lambda e: e.tensor_scalar(out=ss[:, :, 0:4], in0=ss[:, :, 0:4], scalar1=0.125, scalar2=None, op0=ALU.mult), r=[ss], w=[ss])
                        for n in range(NB):
                            K.op("dve", lambda e: e.tensor_tensor(out=qkn[:, n, :, :], in0=xqk[:, n, :, :],
                                                                  in1=ss[:, n, :].rearrange("p (h o) -> p h o", o=1).to_broadcast([64, 8, 64]), op=ALU.mult), r=[x_, ss], w=[qkn])
                        K.op("act", lambda e: e.activation(out=eb[:, :, :], in_=gi[:, :, 256:260], func=AF.Exp, scale=-1.0), r=[gi], w=[eb])
                        K.op("dve", lambda e: e.tensor_scalar(out=eb[:, :, :], in0=eb[:, :, :], scalar1=1.0, scalar2=None, op0=ALU.add), r=[eb], w=[eb])
                        K.op("dve", lambda e: e.reciprocal(out=beta[:, :, :], in_=eb[:, :, :]), r=[eb], w=[beta])
                        K.op("act", lambda e: e.activation(out=lnb[:, :, :], in_=beta[:, :, :], func=AF.Ln), r=[beta], w=[lnb])
                        K.op("dve", lambda e: e.tensor_tensor(out=g_[:, :, :], in0=gi[:, :, 260:264], in1=dtb[:, :].rearrange("p (o h) -> p o h", o=1).to_broadcast([64, NB, 4]),
                                                              op=ALU.add), r=[gi, dtb], w=[g_])
                        K.op("act", lambda e: e.activation(out=g_[:, :, :], in_=g_[:, :, :], func=AF.Exp), r=[g_], w=[g_])
                        K.op("dve", lambda e: e.tensor_scalar(out=g_[:, :, :], in0=g_[:, :, :], scalar1=1.0, scalar2=None, op0=ALU.add), r=[g_], w=[g_])
                        K.op("act", lambda e: e.activation(out=g_[:, :, :], in_=g_[:, :, :], func=AF.Ln), r=[g_], w=[g_])
                        K.op("dve", lambda e: e.tensor_tensor(out=g_[:, :, :], in0=g_[:, :, :], in1=nA[:, :].rearrange("p (o h) -> p o h", o=1).to_broadcast([64, NB, 4]),
                                                              op=ALU.mult), r=[g_, nA], w=[g_])
                        gfl = g_[:, :, :].rearrange("p n h -> p (n h)")
                        W4 = NB * 4
                        K.op("pe", lambda e: e.matmul(bU[:, 0:W4], lhsT=CF[0:64, o_cum:o_cum + 64], rhs=gfl, start=True, stop=True), r=[cst, g_], w=[bU])
                        K.op("pe", lambda e: e.matmul(bU[:, W4:2 * W4], lhsT=CF[0:64, o_lc:o_lc + 64], rhs=gfl, start=True, stop=True), r=[cst, g_], w=[bU])
                        K.op("pe", lambda e: e.matmul(bU[:, 2 * W4:3 * W4], lhsT=CF[0:64, o_on:o_on + 64], rhs=gfl, start=True, stop=True), r=[cst, g_], w=[bU])
                        K.op("dve", lambda e: e.tensor_copy(out=G[:, :, :].rearrange("p n h -> p (n h)"), in_=bU[:, 0:W4]), r=[bU], w=[G])
                        K.op("act", lambda e: e.activation(out=eG[:, :, :, :].rearrange("p a n h -> p (a n h)"), in_=bU[:, 0:3 * W4], func=AF.Exp), r=[bU], w=[eG])
                        for n in range(NB):
                            K.op("dve", lambda e: e.tensor_copy(out=gcols[:, 0, n, :, :], in_=g_[:, n, :].rearrange("p (h o) -> p h o", o=1).to_broadcast([64, 4, 64])), r=[g_], w=[gcols])
                            K.op("dve", lambda e: e.tensor_copy(out=gcols[:, 1, n, :, :], in_=lnb[:, n, :].rearrange("p (h o) -> p h o", o=1).to_broadcast([64, 4, 64])), r=[lnb], w=[gcols])
                        for n in range(NB):
                            for h in range(4):
                                K.op("pe", lambda e: e.matmul(bD5[:, n, h, 0, :], lhsT=gcols[:, 0, n, h, :], rhs=CF[0:64, o_cum:o_cum + 64], start=True, stop=True), r=[gcols, cst], w=[bD])
                                K.op("pe", lambda e: e.matmul(bD5[:, n, h, 1, :], lhsT=gcols[:, 0, n, h, :], rhs=CF[0:64, o_cum:o_cum + 64], start=True, stop=False), r=[gcols, cst], w=[bD])
                                K.op("pe", lambda e: e.matmul(bD5[:, n, h, 1, :], lhsT=gcols[:, 1, n, h, :], rhs=identf, start=False, stop=True), r=[gcols, cst], w=[bD])
                        for n in range(NB):
                            for sl in range(2):
                                K.op("dve", lambda e: e.tensor_tensor(out=DT[:, n, :, sl, :], in0=bD5[:, n, :, sl, :],
                                                                      in1=G[:, n, :].rearrange("p (h o) -> p h o", o=1).to_broadcast([64, 4, 64]), op=ALU.subtract), r=[bD, G], w=[DT])
                        K.op("dve", lambda e: e.tensor_tensor(out=DT[:, :, :, :, :].rearrange("p n h a t -> p n (h a t)"), in0=DT[:, :, :, :, :].rearrange("p n h a t -> p n (h a t)"),
                                                              in1=CF[0:64, o_cap:o_cap + 512].rearrange("p (o x) -> p o x", o=1).to_broadcast([64, NB, 512]), op=ALU.min), r=[DT, cst], w=[DT])
                        K.op("act", lambda e: e.activation(out=LF[:, :, :, :, :].rearrange("p n h a t -> p (n h a t)"), in_=DT[:, :, :, :, :].rearrange("p n h a t -> p (n h a t)"),
                                                           func=AF.Exp), r=[DT], w=[LF])
                        for n in range(NB):
                            bc = lambda t2: t2.rearrange("p (h o) -> p h o", o=1).to_broadcast([64, 4, 64])
                            K.op("dve", lambda e: e.tensor_tensor(out=qdec[:, n, :, :], in0=qkn[:, n, 0:4, :], in1=bc(eG[:, 0, n, :]), op=ALU.mult), r=[qkn, eG], w=[qdec])
                            K.op("dve", lambda e: e.tensor_tensor(out=tA4[:, n, :, :], in0=qkn[:, n, 4:8, :], in1=bc(beta[:, n, :]), op=ALU.mult), r=[qkn, beta], w=[tA4])
                            K.op("dve", lambda e: e.tensor_tensor(out=kbg[:, n, :, :], in0=tA4[:, n, :, :], in1=bc(eG[:, 0, n, :]), op=ALU.mult), r=[tA4, eG], w=[kbg])
                            K.op("pool", lambda e: e.tensor_tensor(out=kdec[:, n, :, :], in0=qkn[:, n, 4:8, :], in1=bc(eG[:, 1, n, :]), op=ALU.mult), r=[qkn, eG], w=[kdec])
                            K.op("pool", lambda e: e.tensor_tensor(out=vbt[:, n, :, :], in0=xv[:, n, :, :], in1=bc(beta[:, n, :]), op=ALU.mult), r=[x_, beta], w=[vbt])
                        for n in range(NB):
                            for a in range(3):
                                for h in range(4):
                                    src = qkn[:, n, h, :] if a == 0 else (qkn[:, n, 4 + h, :] if a == 1 else qdec[:, n, h, :])
                                    s_ = n * 12 + a * 4 + h
                                    K.op("pe", lambda e: e.transpose(bTf[:, s_ * 64:(s_ + 1) * 64], src, identf), r=[qkn, qdec, cst], w=[bT])
                        K.op("act", lambda e: e.copy(out=trs[:, :, :, :, :].rearrange("p n a h t -> p (n a h t)"), in_=bTf[:, 0:NB * 768]), r=[bT], w=[trs])
                        for n in range(NB):
                            K.op("pool", lambda e: e.tensor_copy(out=qkT[:, n, :, :, :], in_=trs[:, n, 0:2, :, :].rearrange("p a h t -> p h a t")), r=[trs], w=[qkT])
                        for n in range(NB):
                            for h in range(4):
                                K.op("pe", lambda e: e.matmul(bD5[:, n, h, :, :], lhsT=trs[:, n, 1, h, :], rhs=qkT[:, n, h, :, :], start=True, stop=True), r=[trs, qkT], w=[bD])
                        K.op("dve", lambda e: e.tensor_tensor(out=Mx[:, :, :, :, :].rearrange("p n h a t -> p n (h a t)"), in0=bD[:, :, :],
                                                              in1=LF[:, :, :, :, :].rearrange("p n h a t -> p n (h a t)"), op=ALU.mult), r=[bD, LF], w=[Mx])
                        X0 = XX[0]
                        for n in range(NB):
                            K.op("act", lambda e: e.copy(out=QKm[:, n, :, :], in_=Mx[:, n, :, 0, :]), r=[Mx], w=[QKm])
                            K.op("dve", lambda e: e.tensor_scalar(out=X0[:, 0, n, :, :], in0=Mx[:, n, :, 1, :], scalar1=-1.0, scalar2=None, op0=ALU.mult), r=[Mx], w=[X0])
                            K.op("pool", lambda e: e.scalar_tensor_tensor(out=Tm[:, n, :, :], in0=Mx[:, n, :, 1, :], scalar=-1.0,
                                                                         in1=identf.rearrange("p (o t) -> p o t", o=1).to_broadcast([64, 4, 64]), op0=ALU.mult, op1=ALU.add),
                                 r=[Mx, cst], w=[Tm]) if False else \
                                K.op("dve", lambda e: e.scalar_tensor_tensor(out=Tm[:, n, :, :], in0=Mx[:, n, :, 1, :], scalar=-1.0,
                                                                             in1=identf.rearrange("p (o t) -> p o t", o=1).to_broadcast([64, 4, 64]), op0=ALU.mult, op1=ALU.add),
                                     r=[Mx, cst], w=[Tm])
                        for n in range(NB):
                            for h in range(4):
                                K.op("pe", lambda e: e.transpose(bTf[:, (n * 4 + h) * 64:(n * 4 + h + 1) * 64], X0[:, 0, n, h, :], identf), r=[X0, cst], w=[bT])
                        K.op("act", lambda e: e.copy(out=X0[:, 1, :, :, :].rearrange("p n h t -> p (n h t)"), in_=bTf[:, 0:NB * 256]), r=[bT], w=[X0])
                        for step in range(5):
                            last = step == 4
                            cur = XX[step % 2]
                            nxt = XX[(step + 1) % 2]
                            for n in range(NB):
                                for h in range(4):
                                    c0 = (n * 4 + h) * 64
                                    K.op("pe", lambda e: e.matmul(bX[:, 1, c0:c0 + 64], lhsT=cur[:, 0, n, h, :], rhs=cur[:, 1, n, h, :], start=True, stop=True), r=[cur], w=[bX])
                                    if not last:
                                        K.op("pe", lambda e: e.matmul(bX[:, 0, c0:c0 + 64], lhsT=cur[:, 1, n, h, :], rhs=cur[:, 0, n, h, :], start=True, stop=True), r=[cur], w=[bX])
                            if last:
                                K.op("act", lambda e: e.copy(out=nxt[:, 1, :, :, :].rearrange("p n h t -> p (n h t)"), in_=bX[:, 1, 0:NB * 256]), r=[bX], w=[nxt])
                            else:
                                K.op("act", lambda e: e.copy(out=nxt[:, :, :, :, :].rearrange("p a n h t -> p a (n h t)"), in_=bX[:, :, 0:NB * 256]), r=[bX], w=[nxt])
                            for n in range(NB):
                                for h in range(4):
                                    c0 = (n * 4 + h) * 64
                                    K.op("pe", lambda e: e.matmul(bU[:, c0:c0 + 64], lhsT=nxt[:, 1, n, h, :], rhs=Tm[:, n, h, :], start=True, stop=True), r=[nxt, Tm], w=[bU])
                            K.op("dve", lambda e: e.tensor_tensor(out=Tm[:, :, :, :].rearrange("p n h t -> p (n h t)"), in0=Tm[:, :, :, :].rearrange("p n h t -> p (n h t)"),
                                                                  in1=bU[:, 0:NB * 256], op=ALU.add), r=[Tm, bU], w=[Tm])
                        for n in range(NB):
                            for h in range(4):
                                c0 = (n * 4 + h) * 64
                                K.op("pe", lambda e: e.matmul(bU[:, c0:c0 + 64], lhsT=kbg[:, n, h, :], rhs=Tm[:, n, h, :], start=True, stop=True), r=[kbg, Tm], w=[bU])
                        K.op("dve", lambda e: e.tensor_scalar(out=nWT[:, :, :, :].rearrange("p n h t -> p (n h t)"), in0=bU[:, 0:NB * 256], scalar1=-1.0, scalar2=None, op0=ALU.mult),
                             r=[bU], w=[nWT])
                        for n in range(NB):
                            for h in range(4):
                                K.op("pe", lambda e: e.matmul(bT[:, 1, h * 64:(h + 1) * 64], lhsT=Tm[:, n, h, :], rhs=vbt[:, n, h, :], start=True, stop=False), r=[Tm, vbt], w=[bT])
                                K.op("pe", lambda e: e.matmul(bT[:, 1, h * 64:(h + 1) * 64], lhsT=nWT[:, n, h, :], rhs=Sf[:, h, :], start=False, stop=True), r=[nWT, Sf], w=[bT])
                            K.op("act", lambda e: e.copy(out=vnew[:, :, :].rearrange("p h d -> p (h d)"), in_=bT[:, 1, 0:256]), r=[bT], w=[vnew])
                            for h in range(4):
                                c0 = (n * 4 + h) * 64
                                K.op("pe", lambda e: e.matmul(bT[:, 2, c0:c0 + 64], lhsT=trs[:, n, 2, h, :], rhs=Sf[:, h, :], start=True, stop=False), r=[trs, Sf], w=[bT])
                                K.op("pe", lambda e: e.matmul(bT[:, 2, c0:c0 + 64], lhsT=QKm[:, n, h, :], rhs=vnew[:, h, :], start=False, stop=True), r=[QKm, vnew], w=[bT])
                            for h in range(4):
                                K.op("pe", lambda e: e.matmul(bT[:, 1, 256 + h * 64:256 + (h + 1) * 64], lhsT=kdec[:, n, h, :], rhs=vnew[:, h, :], start=True, stop=True), r=[kdec, vnew], w=[bT])
                            K.op("dve", lambda e: e.tensor_tensor(out=Sf[:, :, :], in0=Sf[:, :, :], in1=eG[:, 2, n, :].rearrange("p (h o) -> p h o", o=1).to_broadcast([64, 4, 64]),
                                                                  op=ALU.mult), r=[Sf, eG], w=[Sf])
                            K.op("dve", lambda e: e.tensor_tensor(out=Sf[:, :, :], in0=Sf[:, :, :], in1=bT[:, 1, 256:512].rearrange("p (h d) -> p h d", d=64), op=ALU.add),
                                 r=[Sf, bT], w=[Sf])
                        o3 = bT[:, 2, 0:NB * 256].rearrange("p (m d) -> p m d", d=64)
                        tA3 = tA[:, :, :].rearrange("p n (h d) -> p (n h) d", d=64)
                        K.op("act", lambda e: e.activation(out=tA3, in_=o3, func=AF.Square), r=[bT], w=[tA])
                        K.op("dve", lambda e: e.tensor_reduce(out=ssq[:, :, :].rearrange("p n h -> p (n h)"), in_=tA3, axis=AX.X, op=ALU.add), r=[tA], w=[ssq])
                        K.op("act", lambda e: e.activation(out=ssq[:, :, :], in_=ssq[:, :, :], func=AF.Ln, bias=epsb[0:64, :], scale=1.0 / 64), r=[ssq, epsb], w=[ssq])
                        K.op("act", lambda e: e.activation(out=ssq[:, :, :], in_=ssq[:, :, :], func=AF.Exp, scale=-0.5), r=[ssq], w=[ssq])
                        K.op("dve", lambda e: e.tensor_tensor(out=tA3, in0=o3, in1=ssq[:, :, :].rearrange("p n (h o) -> p (n h) o", o=1).to_broadcast([64, NB * 4, 64]), op=ALU.mult),
                             r=[bT, ssq], w=[tA])
                        K.op("dve", lambda e: e.tensor_tensor(out=tA[:, :, :], in0=tA[:, :, :], in1=gnb[:, :].rearrange("p (o x) -> p o x", o=1).to_broadcast([64, NB, 256]), op=ALU.mult),
                             r=[tA, gnb], w=[tA])
                        K.op("act", lambda e: e.activation(out=tB[:, :, :], in_=gi[:, :, 0:256], func=AF.Exp, scale=-1.0), r=[gi], w=[tB])
                        K.op("dve", lambda e: e.tensor_scalar(out=tB[:, :, :], in0=tB[:, :, :], scalar1=1.0, scalar2=None, op0=ALU.add), r=[tB], w=[tB])
                        K.op("dve", lambda e: e.reciprocal(out=tB[:, :, :], in_=tB[:, :, :]), r=[tB], w=[tB])
                        K.op("dve", lambda e: e.tensor_tensor(out=tB[:, :, :], in0=tB[:, :, :], in1=gi[:, :, 0:256], op=ALU.mult), r=[tB, gi], w=[tB])
                        o_ = ob[gi_ % 2]
                        K.op("dve", lambda e: e.tensor_tensor(out=o_[:, :, :], in0=tA[:, :, :], in1=tB[:, :, :], op=ALU.mult), r=[tA, tB], w=[o_])
                        K.dma("pool", "st", mixed_d[r0:r0 + 64 * NB, 768:1024].rearrange("(n p) c -> p n c", p=64), o_[:, :, :], r=[o_])
                    K.phase_end()

            if run("m5"):
                with ExitStack() as st:
                    wo = SB(st, "m5w", [128, 8, D_], BF16)
                    load_weight_bf16(st, "m5w", lambda kc: wout_d[l, kc * 128:(kc + 1) * 128, :], 8, D_, wo, 512)
                    mk = [SB(st, "m5m%d" % i, [128, D_], BF16) for i in range(2)]
                    mT = SB(st, "m5mT", [128, 8, 512], BF16)
                    xt = [SB(st, "m5x%d" % i, [128, 8, 512], F32) for i in range(2)]
                    ptr = [PS(st, "m5pt%d" % i, [128, 8, 128], BF16) for i in range(2)]
                    py = [PS(st, "m5py%d" % i, [128, 512], F32) for i in range(2)]
                    n = 0
                    for ti in range(S_ // 512):
                        x_ = xt[ti % 2]
                        K.dma("sp", "ld", x_[:, :, :], xT_v[:, :, ti * 512:(ti + 1) * 512], w=[x_])
                        for sub in range(4):
                            m_ = mk[n % 2]
                            p = ptr[n % 2]
                            r0 = ti * 512 + sub * 128
                            K.dma("sp", "ld", m_[:, :], mixed_d[r0:r0 + 128, :], w=[m_])
                            for c in range(8):
                                K.op("pe", lambda e: e.transpose(p[:, c, :], m_[:, c * 128:(c + 1) * 128], CB[:, o_id:o_id + 128]), r=[m_, cstb], w=[p])
                            if n % 2:
                                K.op("act", lambda e: e.copy(out=mT[:, :, sub * 128:(sub + 1) * 128], in_=p[:, :, :]), r=[p], w=[mT])
                            else:
                                K.op("dve", lambda e: e.tensor_copy(out=mT[:, :, sub * 128:(sub + 1) * 128], in_=p[:, :, :]), r=[p], w=[mT])
                            n += 1
                        for dc in range(8):
                            p = py[dc % 2]
                            for kc in range(8):
                                K.op("pe", lambda e: e.matmul(p[:, :], lhsT=wo[:, kc, dc * 128:(dc + 1) * 128], rhs=mT[:, kc, :], start=(kc == 0), stop=(kc == 7)),
                                     r=[wo, mT], w=[p], inc=(kc == 7))
                            K.op("dve", lambda e: e.scalar_tensor_tensor(out=x_[:, dc, :], in0=p[:, :], scalar=modt[:, 16 + dc:17 + dc], in1=x_[:, dc, :], op0=ALU.mult, op1=ALU.add),
                                 r=[p, modt, x_], w=[x_])
                        K.dma("pool", "st", xT_v[:, :, ti * 512:(ti + 1) * 512], x_[:, :, :], r=[x_])
                K.phase_end()

            if run("ffn"):
                with ExitStack() as st:
                    wg = SB(st, "fwg", [128, 8, DFF], BF16)
                    wu = SB(st, "fwu", [128, 8, DFF], BF16)
                    wd = SB(st, "fwd", [128, 22, D_], BF16)
                    with ExitStack() as st2:
                        load_weight_bf16(st2, "fwg", lambda kc: wg_d[l, kc * 128:(kc + 1) * 128, :], 8, DFF, wg, 704)
                        load_weight_bf16(st2, "fwu", lambda kc: wu_d[l, kc * 128:(kc + 1) * 128, :], 8, DFF, wu, 704)
                        load_weight_bf16(st2, "fwd", lambda kc: wd_d[l, kc * 128:(kc + 1) * 128, :], 22, D_, wd, 512)
                        K.phase_end()
                    TF = 256
                    xt = [SB(st, "fx%d" % i, [128, 8, TF], F32) for i in range(2)]
                    hT = [SB(st, "fh%d" % i, [128, 8, TF], BF16) for i in range(2)]
                    sq = [SB(st, "fsq%d" % i, [128, 8, TF], BF16) for i in range(2)]
                    rstd = [SB(st, "fr%d" % i, [128, TF], F32) for i in range(2)]
                    tmpf = [SB(st, "ft%d" % i, [128, TF], F32) for i in range(2)]
                    aT = SB(st, "fa", [128, 22, TF], BF16)
                    sg = [SB(st, "fsg%d" % i, [128, TF], F32) for i in range(2)]
                    pss = PS(st, "fpss", [128, 512], F32)
                    pg = [PS(st, "fpg%d" % i, [128, 512], F32) for i in range(2)]
                    pu = [PS(st, "fpu%d" % i, [128, 512], F32) for i in range(2)]
                    py = [PS(st, "fpy%d" % i, [128, 512], F32) for i in range(2)]
                    NT = S_ // TF

                    def f_load(ti):
                        x_ = xt[ti % 2]
                        K.dma("sp", "ld", x_[:, :, :], xT_v[:, :, ti * TF:(ti + 1) * TF], w=[x_])

                    def f_rms(ti):
                        x_ = xt[ti % 2]
                        h_ = hT[ti % 2]
                        s_ = sq[ti % 2]
                        r_ = rstd[ti % 2]
                        K.op("act", lambda e: e.activation(out=s_[:, :, :], in_=x_[:, :, :], func=AF.Square), r=[x_], w=[s_])
                        for c in range(8):
                            K.op("pe", lambda e: e.matmul(pss[:, 0:TF], lhsT=onesb[:, :], rhs=s_[:, c, :], start=(c == 0), stop=(c == 7)), r=[onesb, s_], w=[pss], inc=(c == 7))
                        K.op("act", lambda e: e.activation(out=r_[:, :], in_=pss[:, 0:TF], func=AF.Sqrt, bias=epsb[:, :], scale=1.0 / D_), r=[pss, epsb], w=[r_])
                        K.op("dve", lambda e: e.reciprocal(out=r_[:, :], in_=r_[:, :]), r=[r_], w=[r_])
                        for c in range(8):
                            t = tmpf[c % 2]
                            K.op("dve", lambda e: e.tensor_tensor(out=t[:, :], in0=x_[:, c, :], in1=r_[:, :], op=ALU.mult), r=[x_, r_], w=[t])
                            K.op("act", lambda e: e.activation(out=h_[:, c, :], in_=t[:, :], func=AF.Identity, bias=modt[:, 24 + c:25 + c], scale=gscf[:, c:c + 1]),
                                 r=[t, modt, gscf], w=[h_])

                    f_load(0)
                    f_rms(0)
                    for ti in range(NT):
                        x_ = xt[ti % 2]
                        h_ = hT[ti % 2]
                        if ti + 1 < NT:
                            f_load(ti + 1)
                        for fc in range(22):
                            g_ = pg[fc % 2]
                            u_ = pu[fc % 2]
                            s_ = sg[fc % 2]
                            for kc in range(8):
                                K.op("pe", lambda e: e.matmul(g_[:, 0:TF], lhsT=wg[:, kc, fc * 128:(fc + 1) * 128], rhs=h_[:, kc, :], start=(kc == 0), stop=(kc == 7)), r=[wg, h_], w=[g_], inc=(kc == 7))
                            for kc in range(8):
                                K.op("pe", lambda e: e.matmul(u_[:, 0:TF], lhsT=wu[:, kc, fc * 128:(fc + 1) * 128], rhs=h_[:, kc, :], start=(kc == 0), stop=(kc == 7)), r=[wu, h_], w=[u_], inc=(kc == 7))
                            K.op("act", lambda e: e.activation(out=s_[:, :], in_=g_[:, 0:TF], func=AF.Silu), r=[g_], w=[s_])
                            K.op("dve", lambda e: e.tensor_tensor(out=aT[:, fc, :], in0=s_[:, :], in1=u_[:, 0:TF], op=ALU.mult), r=[s_, u_], w=[aT])
                        if ti + 1 < NT:
                            f_rms(ti + 1)
                        for dc in range(8):
                            p = py[dc % 2]
                            for fc in range(22):
                                K.op("pe", lambda e: e.matmul(p[:, 0:TF], lhsT=wd[:, fc, dc * 128:(dc + 1) * 128], rhs=aT[:, fc, :], start=(fc == 0), stop=(fc == 21)), r=[wd, aT], w=[p], inc=(fc == 21))
                            K.op("dve", lambda e: e.scalar_tensor_tensor(out=x_[:, dc, :], in0=p[:, 0:TF], scalar=modt[:, 40 + dc:41 + dc], in1=x_[:, dc, :], op0=ALU.mult, op1=ALU.add),
                                 r=[p, modt, x_], w=[x_])
                        K.dma("pool", "st", xT_v[:, :, ti * TF:(ti + 1) * TF], x_[:, :, :], r=[x_])
                K.phase_end()

        if run("pout"):
            with ExitStack() as st:
                xin = [SB(st, "pox%d" % i, [128, 8, 128], F32) for i in range(2)]
                xo = [SB(st, "poo%d" % i, [128, D_], F32) for i in range(2)]
                pt = [PS(st, "pop%d" % i, [128, D_], F32) for i in range(2)]
                last = None
                for i in range(S_ // 128):
                    a = xin[i % 2]
                    o = xo[i % 2]
                    p = pt[i % 2]
                    K.dma("sp", "ld", a[:, :, :], xT_d.rearrange("(c p) t -> p c t", p=128)[:, :, i * 128:(i + 1) * 128], w=[a])
                    for c in range(8):
                        K.op("pe", lambda e: e.transpose(p[:, c * 128:(c + 1) * 128], a[:, c, :], CF[:, COFF["ident"][0]:COFF["ident"][0] + 128]), r=[a, cst], w=[p])
                    if i % 2:
                        K.op("act", lambda e: e.copy(out=o[:, :], in_=p[:, :]), r=[p], w=[o])
                    else:
                        K.op("dve", lambda e: e.tensor_copy(out=o[:, :], in_=p[:, :]), r=[p], w=[o])
                    last = K.dma("pool", "st", out_d[i * 128:(i + 1) * 128, :], o[:, :], r=[o])
        K.phase_end()
        print("instructions emitted:", K.ninstr, "sems:", K.nsem)
    return nc


def make_in_maps(inputs):
    f = lambda a: np.ascontiguousarray(np.asarray(a, dtype=np.float32))
    x = f(inputs["x"])
    c = f(inputs["c"])
    pos = np.ascontiguousarray(np.asarray(inputs["positions"], dtype=np.int32))
    fm = lambda v: np.ascontiguousarray(v.reshape(-1, 128).T)
    ada_b = f(inputs["ada_b"])
    adabT = np.stack([fm(ada_b[l]) for l in range(NL)])
    nmixT = np.stack([fm(f(inputs["norm_mix"])[l]) for l in range(NL)])
    nffnT = np.stack([fm(f(inputs["norm_ffn"])[l]) for l in range(NL)])
    small = np.zeros((NL, 1, 1024), np.float32)
    for l in range(NL):
        small[l, 0, 0:64] = f(inputs["attn_q_norm"])[l]
        small[l, 0, 64:128] = f(inputs["attn_k_norm"])[l]
        small[l, 0, 128:136] = f(inputs["attn_sinks"])[l]
        small[l, 0, 136:392] = f(inputs["hgrn_lb_logits"])[l]
        small[l, 0, 648:712] = f(inputs["hgrn_out_norm"])[l]
        small[l, 0, 712:716] = f(inputs["gdn_a_log"])[l]
        small[l, 0, 716:720] = f(inputs["gdn_dt_bias"])[l]
        small[l, 0, 720:784] = f(inputs["gdn_out_norm"])[l]
    cw = f(inputs["gdn_conv_w"])
    convT = np.zeros((NL, 128, 26), np.float32)
    convT[:, :, 0:24] = cw.reshape(NL, 4, 6, 128).transpose(0, 3, 2, 1).reshape(NL, 128, 24)
    convT[:, 0:64, 24] = f(inputs["attn_q_norm"])
    convT[:, 0:64, 25] = f(inputs["attn_k_norm"])
    shared = {
        "ada_w": f(inputs["ada_w"]), "ada_bT": adabT, "nmixT": nmixT, "nffnT": nffnT,
        "w_in": f(inputs["w_in"]), "w_out": f(inputs["w_out"]), "w_gate": f(inputs["w_gate"]),
        "w_up": f(inputs["w_up"]), "w_down": f(inputs["w_down"]), "small": small, "convT": convT, "cst": CST,
    }
    maps = []
    for b in range(8):
        m = dict(shared)
        m["x"] = np.ascontiguousarray(x[b])
        m["cT"] = fm(c[b])
        m["pos"] = np.ascontiguousarray(pos[b:b + 1])
        maps.append(m)
    return maps


def kernel(**inputs):
    nc = build()
    maps = make_in_maps(inputs)
    res = run_bass_kernel_spmd(nc, maps, core_ids=list(range(8)))
    return np.stack([np.asarray(r["out"], dtype=np.float32) for r in res.results], axis=0)
```
